# Optimizing a Trainium2 kernel written in Bass

```python
import jax, jax.numpy as jnp
from jax import lax
import numpy as np

D_MODEL = 4096
BATCH = 1
SEQ = 8192
DEPTH = 4

HEAD_DIM = 128
MIX_WIDTH = D_MODEL
SB_WIDTH = MIX_WIDTH // 2
SB_HEADS = SB_WIDTH // HEAD_DIM
HG_WIDTH = MIX_WIDTH - SB_WIDTH
HG_HEADS = HG_WIDTH // HEAD_DIM
HG_KDIM = HEAD_DIM
HG_VDIM = HEAD_DIM
IN_COLS = 3 * SB_WIDTH + 4 * HG_WIDTH
D_FF = 256 * ((8 * D_MODEL // 3 + 255) // 256)
CONV_WIDTH = 3
SB_BLOCK = 128
HG_CHUNK = 64
EPS = 1e-6

kernel_name = "hybrid_stickbreak_hgrn2_convffn"


def rmsnorm(x, w):
    xf = x.astype(jnp.float32)
    y = xf * lax.rsqrt(jnp.mean(xf * xf, axis=-1, keepdims=True) + EPS)
    return (y * w.astype(jnp.float32)).astype(x.dtype)


def head_rmsnorm(o, w, n_heads):
    B, H, T, dh = o.shape
    o = jnp.transpose(o, (0, 2, 1, 3))
    o = o * lax.rsqrt(jnp.mean(o * o, axis=-1, keepdims=True) + EPS)
    o = o * w.astype(jnp.float32).reshape(n_heads, dh)
    return o.reshape(B, T, H * dh)


def split_heads(a, n_heads):
    B, T, C = a.shape
    return jnp.transpose(a.reshape(B, T, n_heads, C // n_heads), (0, 2, 1, 3))


def stick_breaking_attention(q, k, v):
    T = q.shape[2]
    scale = q.shape[-1] ** -0.5
    outs = []
    for blk in range(T // SB_BLOCK):
        start = blk * SB_BLOCK
        end = start + SB_BLOCK
        qb = q[:, :, start:end]
        kb = k[:, :, :end]
        vb = v[:, :, :end]
        z = jnp.einsum('bhqd,bhkd->bhqk', qb, kb) * scale
        qpos = start + jnp.arange(SB_BLOCK)[:, None]
        kpos = jnp.arange(end)[None, :]
        strict = kpos < qpos
        log_keep = jnp.where(strict, jax.nn.log_sigmoid(-z), 0.0)
        log_remain = lax.cumsum(log_keep, axis=3, reverse=True) - log_keep
        log_w = jnp.where(strict, jax.nn.log_sigmoid(z) + log_remain, -jnp.inf)
        outs.append(jnp.einsum('bhqk,bhkd->bhqd', jnp.exp(log_w), vb))
    return jnp.concatenate(outs, axis=2)


def hgrn2_chunked(q, k, log_f, v):
    B, H, T, dk = q.shape
    dv = v.shape[-1]
    nc = T // HG_CHUNK

    def to_chunks(a):
        return jnp.moveaxis(a.reshape(B, H, nc, HG_CHUNK, a.shape[-1]), 2, 0)

    causal = jnp.arange(HG_CHUNK)[:, None] >= jnp.arange(HG_CHUNK)[None, :]

    def step(S, inp):
        qc, kc, gc, vc = inp
        b = jnp.cumsum(gc, axis=2)
        diff = b[:, :, :, None, :] - b[:, :, None, :, :]
        decay = jnp.exp(jnp.where(causal[:, :, None], diff, -jnp.inf))
        scores = jnp.einsum('bhtk,bhtsk,bhsk->bhts', qc, decay, kc)
        o = (jnp.einsum('bhts,bhsv->bhtv', scores, vc)
             + jnp.einsum('bhtk,bhkv->bhtv', qc * jnp.exp(b), S))
        b_last = b[:, :, -1:, :]
        S = (jnp.exp(b_last[:, :, 0, :])[..., None] * S
             + jnp.einsum('bhsk,bhsv->bhkv', kc * jnp.exp(b_last - b), vc))
        return S, o

    S0 = jnp.zeros((B, H, dk, dv), jnp.float32)
    _, o = lax.scan(step, S0, (to_chunks(q), to_chunks(k), to_chunks(log_f), to_chunks(v)))
    return jnp.moveaxis(o, 0, 2).reshape(B, H, T, dv)


def hybrid_mixer(h, w_in_l, sb_norm_l, lb_l, hg_norm_l, w_out_l):
    proj = (h @ w_in_l).astype(jnp.float32)
    cuts = np.cumsum([SB_WIDTH, SB_WIDTH, SB_WIDTH, HG_WIDTH, HG_WIDTH, HG_WIDTH])
    q_sb, k_sb, v_sb, q_hg, f_hg, i_hg, g_hg = jnp.split(proj, cuts, axis=-1)

    o_sb = stick_breaking_attention(split_heads(q_sb, SB_HEADS),
                                    split_heads(k_sb, SB_HEADS),
                                    split_heads(v_sb, SB_HEADS))
    o_sb = head_rmsnorm(o_sb, sb_norm_l, SB_HEADS)

    lb = lb_l.astype(jnp.float32)
    log_f = jnp.logaddexp(jnp.log(lb), jnp.log1p(-lb) + jax.nn.log_sigmoid(f_hg))
    k_hg = (1.0 - lb) * jax.nn.sigmoid(-f_hg)
    o_hg = hgrn2_chunked(split_heads(jax.nn.silu(q_hg), HG_HEADS),
                         split_heads(k_hg, HG_HEADS),
                         split_heads(log_f, HG_HEADS),
                         split_heads(i_hg, HG_HEADS))
    o_hg = head_rmsnorm(o_hg, hg_norm_l, HG_HEADS) * jax.nn.silu(g_hg)

    merged = jnp.concatenate([o_sb, o_hg], axis=-1).astype(h.dtype)
    return merged @ w_out_l


def conv_ffn(h, w_up_l, conv_w_l, conv_b_l, w_down_l):
    u = h @ w_up_l
    T = u.shape[1]
    u_pad = jnp.pad(u, ((0, 0), (CONV_WIDTH - 1, 0), (0, 0)))
    uc = conv_b_l
    for j in range(CONV_WIDTH):
        uc = uc + conv_w_l[j] * u_pad[:, j:j + T]
    gate, up = jnp.split(uc, 2, axis=-1)
    return (jax.nn.silu(gate) * up) @ w_down_l


def setup_inputs(seed: int = 0) -> dict:
    key = jax.random.key(seed)
    ks = jax.random.split(key, 14)
    f32 = jnp.float32
    x = jax.random.normal(ks[0], (BATCH, SEQ, D_MODEL), f32)
    norm1_w = 1.0 + 0.02 * jax.random.normal(ks[1], (DEPTH, D_MODEL), f32)
    w_in = jax.random.normal(ks[2], (DEPTH, D_MODEL, IN_COLS), f32) * D_MODEL ** -0.5
    sb_norm_w = 1.0 + 0.02 * jax.random.normal(ks[3], (DEPTH, SB_WIDTH), f32)
    hg_lb_param = jax.random.normal(ks[4], (DEPTH, HG_WIDTH), f32)
    hg_norm_w = 1.0 + 0.02 * jax.random.normal(ks[5], (DEPTH, HG_WIDTH), f32)
    w_out = jax.random.normal(ks[6], (DEPTH, MIX_WIDTH, D_MODEL), f32) * MIX_WIDTH ** -0.5
    norm2_w = 1.0 + 0.02 * jax.random.normal(ks[7], (DEPTH, D_MODEL), f32)
    w_up = jax.random.normal(ks[8], (DEPTH, D_MODEL, 2 * D_FF), f32) * D_MODEL ** -0.5
    conv_w = jax.random.normal(ks[9], (DEPTH, CONV_WIDTH, 2 * D_FF), f32) * CONV_WIDTH ** -0.5
    conv_b = 0.02 * jax.random.normal(ks[10], (DEPTH, 2 * D_FF), f32)
    w_down = jax.random.normal(ks[11], (DEPTH, D_FF, D_MODEL), f32) * D_FF ** -0.5
    final_norm_w = 1.0 + 0.02 * jax.random.normal(ks[12], (D_MODEL,), f32)
    return {"x": x, "norm1_w": norm1_w, "w_in": w_in, "sb_norm_w": sb_norm_w,
            "hg_lb_param": hg_lb_param, "hg_norm_w": hg_norm_w, "w_out": w_out,
            "norm2_w": norm2_w, "w_up": w_up, "conv_w": conv_w, "conv_b": conv_b,
            "w_down": w_down, "final_norm_w": final_norm_w}


def reference(x, norm1_w, w_in, sb_norm_w, hg_lb_param, hg_norm_w, w_out,
              norm2_w, w_up, conv_w, conv_b, w_down, final_norm_w):
    lb_all = jnp.cumsum(jax.nn.softmax(hg_lb_param.astype(jnp.float32), axis=0), axis=0)
    lb_all = lb_all - lb_all[0:1]
    h = x
    for l in range(DEPTH):
        h = h + hybrid_mixer(rmsnorm(h, norm1_w[l]), w_in[l], sb_norm_w[l], lb_all[l],
                             hg_norm_w[l], w_out[l])
        h = h + conv_ffn(rmsnorm(h, norm2_w[l]), w_up[l], conv_w[l], conv_b[l], w_down[l])
    return rmsnorm(h, final_norm_w)
```

```python
import contextlib
import numpy as np
import ml_dtypes
import concourse.bass as bass
import concourse.mybir as mybir
from concourse.bass_utils import run_bass_kernel_spmd

F32 = mybir.dt.float32
BF16 = mybir.dt.bfloat16
AF = mybir.ActivationFunctionType
ALU = mybir.AluOpType
AX = mybir.AxisListType

D = 4096
T = 8192
DEPTH = 4
NCORE = 8
NT = T // NCORE
KC = D // 128
SBW = 2048
HGW = 2048
INC = 14336
DFF = 11008
EPS = 1e-6

ENGS = ("pe", "act", "dve", "pool", "sp")


class Buf:
    __slots__ = ("name", "writers", "readers", "gen_deps")

    def __init__(self, name):
        self.name = name
        self.writers = []
        self.readers = []
        self.gen_deps = []


class Op:
    __slots__ = ("eng", "fn", "deps", "is_dma", "sem", "val", "has_dep", "idx")

    def __init__(self, eng, fn, is_dma):
        self.eng = eng
        self.fn = fn
        self.deps = []
        self.is_dma = is_dma
        self.sem = None
        self.val = 0
        self.has_dep = False
        self.idx = None


class Prog:
    def __init__(self, nc):
        self.nc = nc
        self.ops = {e: [] for e in ENGS}
        self.all_ops = []
        self.dma_ops = []
        self.last = {e: None for e in ENGS}
        self.dma_since_bar = []

    def op(self, eng, fn, reads=(), writes=(), partial=False, dma=False, extra_deps=()):
        o = Op(eng, fn, dma)
        deps = list(extra_deps)
        for b in reads:
            deps.extend(b.writers)
        for b in writes:
            if b.readers:
                b.gen_deps = list(b.readers) + list(b.writers)
                deps.extend(b.gen_deps)
                b.writers = []
                b.readers = []
            elif not partial:
                b.gen_deps = list(b.writers)
                deps.extend(b.gen_deps)
                b.writers = []
        for b in reads:
            if dma:
                b.readers.append(o)
            else:
                b.readers = [r for r in b.readers if r.is_dma or r.eng != eng] + [o]
        for b in writes:
            if dma:
                b.writers.append(o)
            else:
                b.writers = [w for w in b.writers if w.is_dma or w.eng != eng] + [o]
        seen = set()
        for d in deps:
            if d is None or id(d) in seen or d is o:
                continue
            seen.add(id(d))
            if d.eng == "pe" and eng == "pe" and not d.is_dma and not dma:
                continue
            o.deps.append(d)
            d.has_dep = True
        self.ops[eng].append(o)
        self.all_ops.append(o)
        if dma:
            self.dma_ops.append(o)
            self.dma_since_bar.append(o)
        else:
            self.last[eng] = o
        return o

    def dma(self, eng, out, in_, reads=(), writes=(), partial=False, **kw):
        return self.op(eng, lambda e: e.dma_start(out=out, in_=in_, **kw), reads, writes,
                       partial=partial, dma=True)

    def barrier(self):
        lasts = [self.last[e] for e in ENGS if self.last[e] is not None]
        dmas = list(self.dma_since_bar)
        self.dma_since_bar = []
        for e in ENGS:
            self.op(e, None, extra_deps=lasts + dmas)

    def emit(self, final_wait_eng="sp", loop=False):
        nc = self.nc
        with contextlib.ExitStack() as st:
            esem = {e: st.enter_context(nc.semaphore("s_" + e)) for e in ENGS}
            ecount = {e: 0 for e in ENGS}
            NPOOL = 12
            dpool = {q: [st.enter_context(nc.semaphore(f"d_{q}{i}")) for i in range(NPOOL)]
                     for q in ("sp", "act", "pool")}
            dcount = {q: [0] * NPOOL for q in ("sp", "act", "pool")}
            dnext = {q: 0 for q in ("sp", "act", "pool")}
            for o in self.all_ops:
                if o.is_dma:
                    q = o.eng
                    k = dnext[q] % NPOOL
                    dnext[q] += 1
                    o.sem = dpool[q][k]
                    prev = dcount[q][k]
                    dcount[q][k] += 16
                    o.val = dcount[q][k]
                    o.idx = prev
                elif o.has_dep and o.fn is not None:
                    ecount[o.eng] += 1
                    o.sem = esem[o.eng]
                    o.val = ecount[o.eng]
                elif o.has_dep:
                    o.sem = esem[o.eng]
                    o.val = ecount[o.eng]
            final_waits = [(o.sem, o.val) for o in self.dma_ops]

            with nc.Block() as block:
                def run(engname):
                    def body(eng):
                        waited = {}

                        def w(s, v):
                            if v <= 0 or waited.get(id(s), 0) >= v:
                                return
                            waited[id(s)] = v
                            eng.wait_ge(s, v)
                        for o in self.ops[engname]:
                            if o.is_dma:
                                w(o.sem, o.idx)
                            if len(o.deps) > 4:
                                mx = {}
                                for d in o.deps:
                                    k = id(d.sem)
                                    if k not in mx or mx[k][1] < d.val:
                                        mx[k] = (d.sem, d.val)
                                for (s_, v_) in mx.values():
                                    w(s_, v_)
                            else:
                                for d in o.deps:
                                    w(d.sem, d.val)
                            if o.fn is None:
                                continue
                            ins = o.fn(eng)
                            if o.is_dma:
                                ins.then_inc(o.sem, 16)
                            elif o.has_dep:
                                ins.then_inc(o.sem, 1)
                        if engname == final_wait_eng:
                            for (s, v) in final_waits:
                                w(s, v)
                    return body
                block.tensor(run("pe"))
                block.scalar(run("act"))
                block.vector(run("dve"))
                if self.ops["pool"]:
                    block.gpsimd(run("pool"))
                block.sync(run("sp"))
            if loop:
                nc.all_engine_barrier()
                for sm in list(esem.values()) + [x_ for q in dpool.values() for x_ in q]:
                    nc.sync.sem_clear(sm)
                nc.all_engine_barrier()


class Ctx:
    AW = 51200

    def __init__(self, nc, arena=False):
        self.nc = nc
        self.st = contextlib.ExitStack()
        self.P = Prog(nc)
        self.n = 0
        self.arena = arena
        self.use_stage = arena
        self.wst = None
        self.nwl = 0
        self.pool_eng = "dve" if arena else "pool"
        if arena:
            self.ar = self.st.enter_context(nc.sbuf_tensor("arena", [128, self.AW], F32))
            self.pp = self.st.enter_context(nc.psum_tensor("parena", [128, 4096], F32))
            self.off = 0
            self.poff = 0

    def phase(self):
        self.P.barrier()
        self.off = 0
        self.poff = 0

    def sb(self, shape, dt, name=None):
        self.n += 1
        nm = name or f"t{self.n}"
        if not self.arena:
            t = self.st.enter_context(self.nc.sbuf_tensor(nm, list(shape), dt))
            return t, Buf(nm)
        nel = int(np.prod(shape[1:]))
        words = nel if dt == F32 else (nel + 1) // 2
        assert self.off + words <= self.AW, (nm, self.off, words)
        v = self.ar[0:shape[0], self.off:self.off + words]
        self.off += words
        if dt != F32:
            v = v.bitcast(dt)[:, 0:nel]
        if len(shape) == 3:
            v = v.rearrange("p (a b) -> p a b", b=shape[2])
        return v, Buf(nm)

    def ps(self, shape, dt=F32, name=None):
        self.n += 1
        nm = name or f"p{self.n}"
        if not self.arena:
            t = self.st.enter_context(self.nc.psum_tensor(nm, list(shape), dt))
            return t, Buf(nm)
        words = ((shape[1] + 511) // 512) * 512
        assert self.poff + words <= 4096, (nm, self.poff)
        v = self.pp[:, self.poff:self.poff + shape[1]]
        self.poff += words
        return v, Buf(nm)

    def begin(self):
        self.P = Prog(self.nc)
        self.off = 0
        self.poff = 0
        self.wst = None
        self.nwl = 0

    def end_body(self):
        self.P.emit(loop=True)

    def wload(self, dst, src, k, c, Bw, eng):
        P = self.P
        if not self.use_stage:
            P.dma("pool", dst, src, writes=[Bw], partial=True)
            return
        if self.wst is None:
            self.wst = [self.sb([128, 2048], F32, f"wst{i}") for i in range(2)]
        st, Bst = self.wst[self.nwl % 2]
        q = "sp" if self.nwl % 2 == 0 else "act"
        self.nwl += 1
        v = st[:, 0:k * c].rearrange("p (k c) -> p k c", c=c)
        P.dma(q, v, src, writes=[Bst])
        if eng == "act":
            P.op("act", lambda e: e.copy(out=dst, in_=v), reads=[Bst], writes=[Bw], partial=True)
        else:
            P.op("dve", lambda e: e.tensor_copy(out=dst, in_=v), reads=[Bst], writes=[Bw], partial=True)

    def finish(self):
        self.P.emit()
        self.st.close()


def _bf(a):
    return np.ascontiguousarray(a).astype(ml_dtypes.bfloat16)


def build_A():
    nc = bass.Bass("TRN2", target_bir_lowering=False)
    hT = nc.dram_tensor("hT", [D, NT], F32, kind="ExternalInput").ap()
    w_in = nc.dram_tensor("w_in", [D, INC], F32, kind="ExternalInput").ap()
    n1w = nc.dram_tensor("n1w", [128, KC], F32, kind="ExternalInput").ap()
    lbp = nc.dram_tensor("lbp", [128, 16, DEPTH], F32, kind="ExternalInput").ap()
    lsel = nc.dram_tensor("lsel", [128, 16, DEPTH], F32, kind="ExternalInput").ap()
    qkT = nc.dram_tensor("qkT", [32, 128, NT], BF16, kind="ExternalOutput").ap()
    vtok = nc.dram_tensor("vtok", [NT, SBW], BF16, kind="ExternalOutput").ap()
    hgT = nc.dram_tensor("hgT", [3, 16, 128, NT], F32, kind="ExternalOutput").ap()
    itok = nc.dram_tensor("itok", [NT, HGW], BF16, kind="ExternalOutput").ap()
    ggT = nc.dram_tensor("ggT", [16, 128, NT], F32, kind="ExternalOutput").ap()
    C = Ctx(nc)
    emit_A(C, hT, w_in, n1w, lbp, lsel, qkT, vtok, hgT, itok, ggT)
    C.finish()
    return nc


def emit_A(C, hT, w_in, n1w, lbp, lsel, qkT, vtok, hgT, itok, ggT):
    P = C.P
    ones, Bones = C.sb([128, 128], BF16, "ones")
    n1, Bn1 = C.sb([128, KC], F32, "n1")
    lb, Blb = C.sb([128, 16], F32, "lb")
    oml, Boml = C.sb([128, 16], F32, "oml")
    lbe, Blbe = C.sb([128, 16, DEPTH], F32, "lbe")
    lbs, Blbs = C.sb([128, 16, DEPTH], F32, "lbs")
    lt1, Blt1 = C.sb([128, 16], F32, "lt1")
    lt2, Blt2 = C.sb([128, 16], F32, "lt2")
    hn, Bhn = C.sb([128, KC, NT], BF16, "hn")
    rstd, Brstd = C.sb([128, NT], F32, "rstd")
    NCH = 3
    ch = [C.sb([128, NT], F32, f"ch{i}") for i in range(NCH)]
    sq = [C.sb([128, NT], BF16, f"sq{i}") for i in range(2)]
    NW = 3
    wt = [C.sb([128, KC, 256], BF16, f"wt{i}") for i in range(NW)]
    acc = [C.ps([128, NT], F32, f"acc{i}") for i in range(4)]
    tmp = [C.sb([128, NT], F32, f"tmp{i}") for i in range(2)]
    stg = [C.sb([128, NT], F32, f"stg{i}") for i in range(3)]
    stb = [C.sb([128, NT], BF16, f"stb{i}") for i in range(2)]
    stk = [C.sb([128, NT // 128, 256], BF16, f"stk{i}") for i in range(2)]

    P.op("dve", lambda e: e.memset(ones[:], 1.0), writes=[Bones])
    P.dma("sp", n1[:], n1w[:, :], writes=[Bn1])
    P.dma("sp", lbe[:], lbp[:, :, :], writes=[Blbe])
    P.dma("sp", lbs[:], lsel[:, :, :], writes=[Blbs])
    P.op("act", lambda e: e.activation(out=lbe[:], in_=lbe[:], func=AF.Exp), reads=[Blbe], writes=[Blbe])
    P.op("dve", lambda e: e.tensor_reduce(out=lt1[:], in_=lbe[:], axis=AX.X, op=ALU.add), reads=[Blbe], writes=[Blt1])
    P.op("dve", lambda e: e.tensor_tensor(out=lbs[:], in0=lbe[:], in1=lbs[:], op=ALU.mult), reads=[Blbe, Blbs], writes=[Blbs])
    P.op("dve", lambda e: e.tensor_reduce(out=lt2[:], in_=lbs[:], axis=AX.X, op=ALU.add), reads=[Blbs], writes=[Blt2])
    P.op("dve", lambda e: e.reciprocal(out=lt1[:], in_=lt1[:]), reads=[Blt1], writes=[Blt1])
    P.op("dve", lambda e: e.tensor_tensor(out=lb[:], in0=lt2[:], in1=lt1[:], op=ALU.mult), reads=[Blt1, Blt2], writes=[Blb])
    P.op("dve", lambda e: e.tensor_scalar(out=oml[:], in0=lb[:], scalar1=-1.0, scalar2=1.0, op0=ALU.mult, op1=ALU.add),
         reads=[Blb], writes=[Boml])

    ssq, Bssq = acc[0]
    for kc in range(KC):
        c, Bc = ch[kc % NCH]
        s, Bs = sq[kc % 2]
        P.dma("sp" if kc % 2 == 0 else "act", c[:], hT[kc * 128:(kc + 1) * 128, :], writes=[Bc])
        P.op("act", lambda e, c=c, s=s: e.activation(out=s[:], in_=c[:], func=AF.Square), reads=[Bc], writes=[Bs])
        for hf in range(2):
            P.op("pe", lambda e, s=s, hf=hf, kc=kc: e.matmul(ssq[:, hf * 512:(hf + 1) * 512], lhsT=ones[:], rhs=s[:, hf * 512:(hf + 1) * 512],
                                                          start=(kc == 0), stop=(kc == KC - 1)),
                 reads=[Bs, Bones], writes=[Bssq], partial=True)
    t0, Bt0 = tmp[0]
    P.op("act", lambda e: e.activation(out=t0[:], in_=ssq[:], func=AF.Ln, scale=1.0 / D, bias=EPS), reads=[Bssq], writes=[Bt0])
    P.op("act", lambda e: e.activation(out=rstd[:], in_=t0[:], func=AF.Exp, scale=-0.5), reads=[Bt0], writes=[Brstd])
    for kc in range(KC):
        c, Bc = ch[kc % NCH]
        P.dma("sp" if kc % 2 == 0 else "act", c[:], hT[kc * 128:(kc + 1) * 128, :], writes=[Bc])
        P.op("dve", lambda e, c=c, kc=kc: e.scalar_tensor_tensor(out=hn[:, kc, :], in0=c[:], scalar=n1[:, kc:kc + 1], in1=rstd[:],
                                                                 op0=ALU.mult, op1=ALU.mult),
             reads=[Bc, Bn1, Brstd], writes=[Bhn], partial=True)
    wv = w_in.rearrange("(kc p) c -> p kc c", p=128)
    NG = INC // 256
    na = 0
    nst = 0
    for g in range(NG):
        w_, Bw = wt[g % NW]
        for q4 in range(4):
            C.wload(w_[:, q4 * 8:(q4 + 1) * 8, :], wv[:, q4 * 8:(q4 + 1) * 8, g * 256:(g + 1) * 256], 8, 256, Bw, "act" if g % 2 else "dve")
        col0 = g * 256
        seg = col0 // 2048
        if seg in (2, 5):
            sk, Bsk = stk[nst % 2]
            nst += 1
            for tb in range(NT // 128):
                a, Ba = acc[na % 4]
                na += 1
                for kc in range(KC):
                    P.op("pe", lambda e, a=a, w_=w_, kc=kc, tb=tb: e.matmul(a[:, 0:256], lhsT=hn[:, kc, tb * 128:(tb + 1) * 128], rhs=w_[:, kc, :],
                                                                          start=(kc == 0), stop=(kc == KC - 1)),
                         reads=[Bhn, Bw], writes=[Ba], partial=True)
                if tb % 2 == 0:
                    P.op("act", lambda e, a=a, sk=sk, tb=tb: e.copy(out=sk[:, tb, :], in_=a[:, 0:256]), reads=[Ba], writes=[Bsk], partial=True)
                else:
                    P.op("dve", lambda e, a=a, sk=sk, tb=tb: e.tensor_copy(out=sk[:, tb, :], in_=a[:, 0:256]), reads=[Ba], writes=[Bsk], partial=True)
            dst = vtok if seg == 2 else itok
            c0 = col0 - (4096 if seg == 2 else 10240)
            P.dma("sp", dst[:, c0:c0 + 256].rearrange("(b p) c -> p b c", p=128), sk[:], reads=[Bsk])
            continue
        for cb in range(2):
            a, Ba = acc[na % 4]
            na += 1
            for kc in range(KC):
                for hf in range(2):
                    P.op("pe", lambda e, a=a, w_=w_, kc=kc, hf=hf, cb=cb: e.matmul(a[:, hf * 512:(hf + 1) * 512], lhsT=w_[:, kc, cb * 128:(cb + 1) * 128],
                                                                                 rhs=hn[:, kc, hf * 512:(hf + 1) * 512],
                                                                                 start=(kc == 0), stop=(kc == KC - 1)),
                         reads=[Bhn, Bw], writes=[Ba], partial=True)
            blk = (col0 % 2048) // 128 + cb
            if seg == 0:
                o_, Bo = stb[na % 2]
                P.op("act", lambda e, a=a, o_=o_: e.activation(out=o_[:], in_=a[:], func=AF.Copy, scale=float(128 ** -0.5)), reads=[Ba], writes=[Bo])
                P.dma("sp", qkT[blk], o_[:], reads=[Bo])
            elif seg == 1:
                o_, Bo = stb[na % 2]
                P.op("dve", lambda e, a=a, o_=o_: e.tensor_copy(out=o_[:], in_=a[:]), reads=[Ba], writes=[Bo])
                P.dma("sp", qkT[16 + blk], o_[:], reads=[Bo])
            elif seg in (3, 6):
                t_, Bt = tmp[na % 2]
                o_, Bo = stg[na % 3]
                P.op("act", lambda e, a=a, t_=t_: e.activation(out=t_[:], in_=a[:], func=AF.Exp, scale=-1.0), reads=[Ba], writes=[Bt])
                P.op("dve", lambda e, t_=t_: e.tensor_scalar_add(out=t_[:], in0=t_[:], scalar1=1.0), reads=[Bt], writes=[Bt])
                P.op("dve", lambda e, t_=t_: e.reciprocal(out=t_[:], in_=t_[:]), reads=[Bt], writes=[Bt])
                P.op("dve", lambda e, a=a, t_=t_, o_=o_: e.tensor_tensor(out=o_[:], in0=a[:], in1=t_[:], op=ALU.mult), reads=[Ba, Bt], writes=[Bo])
                P.dma("sp", hgT[0, blk] if seg == 3 else ggT[blk], o_[:], reads=[Bo])
            else:
                t_, Bt = tmp[na % 2]
                o1, Bo1 = stg[0]
                o2, Bo2 = stg[1]
                P.op("act", lambda e, a=a, t_=t_: e.activation(out=t_[:], in_=a[:], func=AF.Exp, scale=-1.0), reads=[Ba], writes=[Bt])
                P.op("dve", lambda e, t_=t_: e.tensor_scalar_add(out=t_[:], in0=t_[:], scalar1=1.0), reads=[Bt], writes=[Bt])
                P.op("dve", lambda e, t_=t_: e.reciprocal(out=t_[:], in_=t_[:]), reads=[Bt], writes=[Bt])
                P.op("dve", lambda e, t_=t_, blk=blk: e.tensor_scalar(out=t_[:], in0=t_[:], scalar1=oml[:, blk:blk + 1], scalar2=lb[:, blk:blk + 1],
                                                                        op0=ALU.mult, op1=ALU.add), reads=[Bt, Boml, Blb], writes=[Bt])
                P.op("act", lambda e, t_=t_, o1=o1: e.activation(out=o1[:], in_=t_[:], func=AF.Ln), reads=[Bt], writes=[Bo1])
                P.op("dve", lambda e, t_=t_, o2=o2: e.tensor_scalar(out=o2[:], in0=t_[:], scalar1=-1.0, scalar2=1.0, op0=ALU.mult, op1=ALU.add),
                     reads=[Bt], writes=[Bo2])
                P.dma("sp", hgT[2, blk], o1[:], reads=[Bo1])
                P.dma("sp", hgT[1, blk], o2[:], reads=[Bo2])


def mixer_consts():
    j = np.arange(128)[:, None]
    s = np.arange(128)[None, :]
    uneg = np.where(j >= s, -1.0, 0.0).astype(np.float32)
    c = np.arange(896)[None, :]
    maskw = ((c - 384) > j).astype(np.float32)
    hmask = ((j // 64 == s // 64) & (j <= s)).astype(np.float32)
    rmask = np.broadcast_to((np.arange(1024) % 64 != 0).astype(np.float32), (128, 1024))
    ident = np.eye(128, dtype=np.float32)
    return {"c_uneg": _bf(uneg), "c_maskw": _bf(maskw), "c_hmask": np.ascontiguousarray(hmask),
            "c_rmask": np.ascontiguousarray(rmask), "c_ident": _bf(ident)}


def build_B(Tn=T):
    nc = bass.Bass("TRN2", target_bir_lowering=False)
    qT = nc.dram_tensor("qT", [2, 128, Tn], BF16, kind="ExternalInput").ap()
    kT = nc.dram_tensor("kT", [2, 128, Tn], BF16, kind="ExternalInput").ap()
    vtok = nc.dram_tensor("vtok", [Tn, 256], BF16, kind="ExternalInput").ap()
    hq = nc.dram_tensor("hq", [2, 128, Tn], F32, kind="ExternalInput").ap()
    hk = nc.dram_tensor("hk", [2, 128, Tn], F32, kind="ExternalInput").ap()
    hg = nc.dram_tensor("hg", [2, 128, Tn], F32, kind="ExternalInput").ap()
    itok = nc.dram_tensor("itok", [Tn, 256], BF16, kind="ExternalInput").ap()
    gg = nc.dram_tensor("gg", [2, 128, Tn], F32, kind="ExternalInput").ap()
    nw = nc.dram_tensor("nw", [128, 4], F32, kind="ExternalInput").ap()
    cst = _const_aps(nc)
    mT = nc.dram_tensor("mT", [4, 128, Tn], BF16, kind="ExternalOutput").ap()
    C = Ctx(nc)
    emit_B(C, Tn, qT, kT, vtok, hq, hk, hg, itok, gg, nw, cst, mT[0:2], mT[2:4])
    C.finish()
    return nc


def _const_aps(nc):
    return (nc.dram_tensor("c_uneg", [128, 128], BF16, kind="ExternalInput").ap(),
            nc.dram_tensor("c_maskw", [128, 896], BF16, kind="ExternalInput").ap(),
            nc.dram_tensor("c_hmask", [128, 128], F32, kind="ExternalInput").ap(),
            nc.dram_tensor("c_rmask", [128, 1024], F32, kind="ExternalInput").ap(),
            nc.dram_tensor("c_ident", [128, 128], BF16, kind="ExternalInput").ap())


def emit_B(C, Tn, qT, kT, vtok, hq, hk, hg, itok, gg, nw, cst, m_sb, m_hg):
    c_uneg, c_maskw, c_hmask, c_rmask, c_ident = cst
    NQT = Tn // 512
    NKB = Tn // 128
    NSEG = Tn // 1024
    P = C.P
    ones, Bones = C.sb([128, 128], BF16, "ones")
    uneg, Buneg = C.sb([128, 128], BF16, "uneg")
    maskw, Bmaskw = C.sb([128, 896], BF16, "maskw")
    hmask, Bhmask = C.sb([128, 128], F32, "hmask")
    rmask, Brmask = C.sb([128, 1024], F32, "rmask")
    ident, Bident = C.sb([128, 128], BF16, "ident")
    nws, Bnws = C.sb([128, 4], F32, "nws")
    P.op("dve", lambda e: e.memset(ones[:], 1.0), writes=[Bones])
    P.dma("sp", uneg[:], c_uneg[:, :], writes=[Buneg])
    P.dma("sp", maskw[:], c_maskw[:, :], writes=[Bmaskw])
    P.dma("sp", hmask[:], c_hmask[:, :], writes=[Bhmask])
    P.dma("sp", rmask[:], c_rmask[:, :], writes=[Brmask])
    P.dma("sp", ident[:], c_ident[:, :], writes=[Bident])
    P.dma("sp", nws[:], nw[:, :], writes=[Bnws])
    pb = [C.ps([128, 512], F32, f"pb{i}") for i in range(8)]

    gS, BgS = C.sb([128, 1024], F32, "gS")
    qS, BqS = C.sb([128, 1024], F32, "qS")
    kS, BkS = C.sb([128, 1024], F32, "kS")
    ggS, BggS = C.sb([128, 1024], F32, "ggS")
    bS, BbS = C.sb([128, 1024], F32, "bS")
    ebS, BebS = C.sb([128, 1024], F32, "ebS")
    enbS, BenbS = C.sb([128, 1024], F32, "enbS")
    khS, BkhS = C.sb([128, 1024], F32, "khS")
    qhB, BqhB = C.sb([128, 1024], BF16, "qhB")
    khB, BkhB = C.sb([128, 1024], BF16, "khB")
    k2B, Bk2B = C.sb([128, 1024], BF16, "k2B")
    vH, BvH = C.sb([128, 8, 128], BF16, "vH")
    oS, BoS = C.sb([128, 1024], F32, "oS")
    sqS, BsqS = C.sb([128, 1024], BF16, "sqS")
    rS, BrS = C.sb([128, 1024], F32, "rS")
    yS, ByS = C.sb([128, 1024], F32, "yS")
    yB, ByB = C.sb([128, 1024], BF16, "yB")
    S32, BS32 = C.sb([128, 128], F32, "S32")
    Sb, BSb = C.sb([128, 128], BF16, "Sb")
    scm = [C.sb([128, 128], BF16, f"scm{i}") for i in range(2)]
    k2T = [C.sb([128, 128], BF16, f"k2T{i}") for i in range(2)]
    (p_sc, Bp_sc), (p_tr, Bp_tr), (p_o, Bp_o), (p_s1, Bp_s1), (p_s2, Bp_s2), (p_n0, Bp_n0), (p_n1, Bp_n1) = pb[0:7]
    p_trb = p_tr[:, 0:64].bitcast(BF16)
    for hh in range(2):
        P.op("dve", lambda e: e.memset(S32[:], 0.0), writes=[BS32])
        P.op("dve", lambda e: e.memset(Sb[:], 0.0), writes=[BSb])
        for sg in range(NSEG):
            t0 = sg * 1024
            P.dma("sp", gS[:], hg[hh, :, t0:t0 + 1024], writes=[BgS])
            P.dma("act", qS[:], hq[hh, :, t0:t0 + 1024], writes=[BqS])
            P.dma("sp", kS[:], hk[hh, :, t0:t0 + 1024], writes=[BkS])
            P.dma("act", ggS[:], gg[hh, :, t0:t0 + 1024], writes=[BggS])
            P.dma("sp", vH[:], itok[t0:t0 + 1024, hh * 128:(hh + 1) * 128].rearrange("(b p) c -> p b c", p=128), writes=[BvH])
            P.op("dve", lambda e: e.tensor_tensor_scan(out=bS[:], data0=rmask[:], data1=gS[:], initial=0.0, op0=ALU.mult, op1=ALU.add),
                 reads=[Brmask, BgS], writes=[BbS])
            P.op("act", lambda e: e.activation(out=ebS[:], in_=bS[:], func=AF.Exp), reads=[BbS], writes=[BebS])
            P.op("act", lambda e: e.activation(out=enbS[:], in_=bS[:], func=AF.Exp, scale=-1.0), reads=[BbS], writes=[BenbS])
            P.op("dve", lambda e: e.tensor_tensor(out=qhB[:], in0=qS[:], in1=ebS[:], op=ALU.mult), reads=[BqS, BebS], writes=[BqhB])
            P.op(C.pool_eng, lambda e: e.tensor_tensor(out=khS[:], in0=kS[:], in1=enbS[:], op=ALU.mult), reads=[BkS, BenbS], writes=[BkhS])
            P.op("act", lambda e: e.copy(out=khB[:], in_=khS[:]), reads=[BkhS], writes=[BkhB])
            for ck in range(16):
                P.op("dve", lambda e, ck=ck: e.tensor_scalar(out=k2B[:, ck * 64:(ck + 1) * 64], in0=khS[:, ck * 64:(ck + 1) * 64],
                                                              scalar1=ebS[:, ck * 64 + 63:ck * 64 + 64], scalar2=None, op0=ALU.mult),
                     reads=[BkhS, BebS], writes=[Bk2B], partial=True)
            for bi in range(8):
                c0 = bi * 128
                sm, Bsm = scm[bi % 2]
                kt, Bkt = k2T[bi % 2]
                P.op("pe", lambda e, c0=c0: e.matmul(p_sc[:, 0:128], lhsT=khB[:, c0:c0 + 128], rhs=qhB[:, c0:c0 + 128], start=True, stop=True),
                     reads=[BkhB, BqhB], writes=[Bp_sc])
                P.op("dve", lambda e, sm=sm: e.tensor_tensor(out=sm[:], in0=p_sc[:, 0:128], in1=hmask[:], op=ALU.mult),
                     reads=[Bp_sc, Bhmask], writes=[Bsm])
                P.op("pe", lambda e, c0=c0: e.transpose(p_trb, k2B[:, c0:c0 + 128], ident[:]), reads=[Bk2B, Bident], writes=[Bp_tr])
                P.op("act", lambda e, kt=kt: e.copy(out=kt[:], in_=p_trb), reads=[Bp_tr], writes=[Bkt])
                P.op("pe", lambda e, sm=sm, bi=bi: e.matmul(p_o[:, 0:128], lhsT=vH[:, bi, :], rhs=sm[:], start=True, stop=False),
                     reads=[BvH, Bsm], writes=[Bp_o])
                for half in range(2):
                    ps_, Bps_ = (p_s1, Bp_s1) if half == 0 else (p_s2, Bp_s2)
                    lo = half * 64
                    P.op("pe", lambda e, c0=c0, lo=lo, half=half: e.matmul(p_o[:, lo:lo + 64], lhsT=Sb[:], rhs=qhB[:, c0 + lo:c0 + lo + 64],
                                                                            start=False, stop=(half == 1)),
                         reads=[BSb, BqhB], writes=[Bp_o], partial=True)
                    P.op("pe", lambda e, kt=kt, bi=bi, lo=lo, ps_=ps_: e.matmul(ps_[:, 0:128], lhsT=kt[lo:lo + 64, :], rhs=vH[lo:lo + 64, bi, :],
                                                                                  start=True, stop=True),
                         reads=[Bkt, BvH], writes=[Bps_])
                    P.op("dve", lambda e, ps_=ps_, c0=c0, lo=lo: e.scalar_tensor_tensor(out=S32[:], in0=S32[:], scalar=ebS[:, c0 + lo + 63:c0 + lo + 64],
                                                                                         in1=ps_[:, 0:128], op0=ALU.mult, op1=ALU.add),
                         reads=[BS32, BebS, Bps_], writes=[BS32])
                    P.op("act", lambda e: e.copy(out=Sb[:], in_=S32[:]), reads=[BS32], writes=[BSb])
                P.op("act", lambda e, c0=c0: e.copy(out=oS[:, c0:c0 + 128], in_=p_o[:, 0:128]), reads=[Bp_o], writes=[BoS], partial=True)
            P.op("act", lambda e: e.activation(out=sqS[:], in_=oS[:], func=AF.Square), reads=[BoS], writes=[BsqS])
            for hf, (pn, Bpn) in enumerate(((p_n0, Bp_n0), (p_n1, Bp_n1))):
                P.op("pe", lambda e, pn=pn, hf=hf: e.matmul(pn[:], lhsT=ones[:], rhs=sqS[:, hf * 512:(hf + 1) * 512], start=True, stop=True),
                     reads=[Bones, BsqS], writes=[Bpn])
                P.op("act", lambda e, pn=pn, hf=hf: e.activation(out=rS[:, hf * 512:(hf + 1) * 512], in_=pn[:], func=AF.Ln, scale=1.0 / 128, bias=EPS),
                     reads=[Bpn], writes=[BrS], partial=True)
            P.op("act", lambda e: e.activation(out=rS[:], in_=rS[:], func=AF.Exp, scale=-0.5), reads=[BrS], writes=[BrS])
            P.op("dve", lambda e, hh=hh: e.scalar_tensor_tensor(out=yS[:], in0=oS[:], scalar=nws[:, 2 + hh:3 + hh], in1=rS[:], op0=ALU.mult, op1=ALU.mult),
                 reads=[BoS, Bnws, BrS], writes=[ByS])
            P.op("dve", lambda e: e.tensor_tensor(out=yB[:], in0=yS[:], in1=ggS[:], op=ALU.mult), reads=[ByS, BggS], writes=[ByB])
            P.dma("sp", m_hg[hh, :, t0:t0 + 1024], yB[:], reads=[ByB])

    qA, BqA = C.sb([128, 2, Tn], BF16, "qA")
    kA, BkA = C.sb([128, 2, Tn], BF16, "kA")
    vA, BvA = C.sb([128, NKB, 256], BF16, "vA")
    for h in range(2):
        P.dma("sp", qA[:, h, :], qT[h], writes=[BqA], partial=True)
        P.dma("act", kA[:, h, :], kT[h], writes=[BkA], partial=True)
    for q4 in range(4):
        n4 = NKB // 4
        P.dma("sp", vA[:, q4 * n4:(q4 + 1) * n4, :], vtok[q4 * n4 * 128:(q4 + 1) * n4 * 128, :].rearrange("(b p) c -> p b c", p=128),
              writes=[BvA], partial=True)
    eS = [C.sb([128, 512], F32, f"eS{i}") for i in range(2)]
    spB = [C.sb([128, 512], BF16, f"spB{i}") for i in range(3)]
    spM = [C.sb([128, 512], BF16, f"spM{i}") for i in range(2)]
    lwS = [C.sb([128, 512], F32, f"lwS{i}") for i in range(2)]
    aB = [C.sb([128, 512], BF16, f"aB{i}") for i in range(3)]
    aM = [C.sb([128, 512], BF16, f"aM{i}") for i in range(2)]
    racc = [C.sb([128, 512], F32, f"racc{i}") for i in range(2)]
    sqA = [C.sb([128, 512], BF16, f"sqA{i}") for i in range(2)]
    rA = [C.sb([128, 512], F32, f"rA{i}") for i in range(2)]
    yA = [C.sb([128, 512], BF16, f"yA{i}") for i in range(2)]
    Pb = pb[0:3]
    Tb = pb[3:5]
    Ob = pb[5:7]
    Nb = pb[7]
    blocks = []
    for h in range(2):
        for qi in range(NQT):
            nkb = 4 * qi + 4
            for kb in reversed(range(nkb)):
                blocks.append((h, qi, kb, kb == nkb - 1, kb == 0))
    st = {}

    def s1(i):
        h, qi, kb, first, last = blocks[i]
        p_, Bp = Pb[i % 3]
        e_, Be = eS[i % 2]
        sp_, Bsp = spB[i % 3]
        qt = (h * NQT + qi)
        if first:
            r_, Br = racc[qt % 2]
            P.op(C.pool_eng, lambda e, r_=r_: e.memset(r_[:], 0.0), writes=[Br])
        P.op("pe", lambda e, p_=p_, h=h, kb=kb, qi=qi: e.matmul(p_[:], lhsT=kA[:, h, kb * 128:(kb + 1) * 128], rhs=qA[:, h, qi * 512:(qi + 1) * 512],
                                                              start=True, stop=False),
             reads=[BkA, BqA], writes=[Bp])
        P.op("act", lambda e, p_=p_, e_=e_: e.activation(out=e_[:], in_=p_[:], func=AF.Exp), reads=[Bp], writes=[Be])
        P.op("act", lambda e, e_=e_, sp_=sp_: e.activation(out=sp_[:], in_=e_[:], func=AF.Ln, bias=1.0), reads=[Be], writes=[Bsp])
        di = kb - 4 * qi
        if di >= 0:
            m_, Bm = spM[i % 2]
            off = 384 - 128 * di
            P.op(C.pool_eng, lambda e, sp_=sp_, m_=m_, off=off: e.tensor_tensor(out=m_[:], in0=sp_[:], in1=maskw[:, off:off + 512], op=ALU.mult),
                 reads=[Bsp, Bmaskw], writes=[Bm])
            st[i] = (m_, Bm)
        else:
            st[i] = (sp_, Bsp)

    def s2(i):
        h, qi, kb, first, last = blocks[i]
        p_, Bp = Pb[i % 3]
        t_, Bt = Tb[i % 2]
        sp_, Bsp = st[i]
        lw_, Blw = lwS[i % 2]
        a_, Ba = aB[i % 3]
        qt = (h * NQT + qi)
        r_, Br = racc[qt % 2]
        P.op("pe", lambda e, p_=p_, sp_=sp_: e.matmul(p_[:], lhsT=uneg[:], rhs=sp_[:], start=False, stop=True),
             reads=[Buneg, Bsp], writes=[Bp], partial=True)
        if not last:
            P.op("pe", lambda e, t_=t_, sp_=sp_: e.matmul(t_[:], lhsT=ones[:], rhs=sp_[:], start=True, stop=True),
                 reads=[Bones, Bsp], writes=[Bt])
        P.op("dve", lambda e, p_=p_, r_=r_, lw_=lw_: e.tensor_tensor(out=lw_[:], in0=p_[:], in1=r_[:], op=ALU.subtract),
             reads=[Bp, Br], writes=[Blw])
        if not last:
            P.op("dve", lambda e, t_=t_, r_=r_: e.tensor_tensor(out=r_[:], in0=t_[:], in1=r_[:], op=ALU.add), reads=[Bt, Br], writes=[Br])
        P.op("act", lambda e, lw_=lw_, a_=a_: e.activation(out=a_[:], in_=lw_[:], func=AF.Exp), reads=[Blw], writes=[Ba])
        di = kb - 4 * qi
        if di >= 0:
            m_, Bm = aM[i % 2]
            off = 384 - 128 * di
            P.op(C.pool_eng, lambda e, a_=a_, m_=m_, off=off: e.tensor_tensor(out=m_[:], in0=a_[:], in1=maskw[:, off:off + 512], op=ALU.mult),
                 reads=[Ba, Bmaskw], writes=[Bm])
            st[i] = (m_, Bm)
        else:
            st[i] = (a_, Ba)

    def s3(i):
        h, qi, kb, first, last = blocks[i]
        a_, Ba = st[i]
        qt = (h * NQT + qi)
        o_, Bo = Ob[qt % 2]
        P.op("pe", lambda e, o_=o_, a_=a_, h=h, kb=kb, first=first, last=last: e.matmul(o_[:], lhsT=vA[:, kb, h * 128:(h + 1) * 128], rhs=a_[:],
                                                                                    start=first, stop=last),
             reads=[BvA, Ba], writes=[Bo], partial=not first)
        if last:
            s_, Bs = sqA[qt % 2]
            rr, Brr = rA[qt % 2]
            y_, By = yA[qt % 2]
            n_, Bn = Nb
            P.op("act", lambda e, o_=o_, s_=s_: e.activation(out=s_[:], in_=o_[:], func=AF.Square), reads=[Bo], writes=[Bs])
            P.op("pe", lambda e, s_=s_: e.matmul(n_[:], lhsT=ones[:], rhs=s_[:], start=True, stop=True), reads=[Bones, Bs], writes=[Bn])
            P.op("act", lambda e, rr=rr: e.activation(out=rr[:], in_=n_[:], func=AF.Ln, scale=1.0 / 128, bias=EPS), reads=[Bn], writes=[Brr])
            P.op("act", lambda e, rr=rr: e.activation(out=rr[:], in_=rr[:], func=AF.Exp, scale=-0.5), reads=[Brr], writes=[Brr])
            P.op("dve", lambda e, o_=o_, rr=rr, y_=y_, h=h: e.scalar_tensor_tensor(out=y_[:], in0=o_[:], scalar=nws[:, h:h + 1], in1=rr[:],
                                                                                 op0=ALU.mult, op1=ALU.mult),
                 reads=[Bo, Bnws, Brr], writes=[By])
            P.dma("sp", m_sb[h, :, qi * 512:(qi + 1) * 512], y_[:], reads=[By])

    nb_ = len(blocks)
    for i in range(nb_ + 2):
        if i < nb_:
            s1(i)
        if 0 <= i - 1 < nb_:
            s2(i - 1)
        if 0 <= i - 2 < nb_:
            s3(i - 2)


NTH = NT + 2
NJ = DFF // 128
DBGK = "Internal"


def build_C():
    nc = bass.Bass("TRN2", target_bir_lowering=False)
    mTh = nc.dram_tensor("mTh", [D, NTH], BF16, kind="ExternalInput").ap()
    hTh = nc.dram_tensor("hTh", [D, NTH], F32, kind="ExternalInput").ap()
    w_out = nc.dram_tensor("w_out", [D, D], F32, kind="ExternalInput").ap()
    n2w = nc.dram_tensor("n2w", [128, KC], F32, kind="ExternalInput").ap()
    w_up = nc.dram_tensor("w_up", [D, 2 * DFF], F32, kind="ExternalInput").ap()
    cw = nc.dram_tensor("cw", [128, 2 * NJ, 3], F32, kind="ExternalInput").ap()
    cbv = nc.dram_tensor("cb", [128, 2 * NJ], F32, kind="ExternalInput").ap()
    w_down = nc.dram_tensor("w_down", [DFF, D], F32, kind="ExternalInput").ap()
    h2T = nc.dram_tensor("h2T", [D, NT], F32, kind="ExternalOutput").ap()
    hmid = nc.dram_tensor("hmid", [D, NTH], F32, kind=DBGK).ap()
    aT = nc.dram_tensor("aT", [NJ, 128, NT], BF16, kind=DBGK).ap()
    C = Ctx(nc)
    emit_C(C, mTh, hTh, w_out, n2w, w_up, cw, cbv, w_down, h2T, hmid, aT)
    C.finish()
    return nc


def emit_C(C, mTh, hTh, w_out, n2w, w_up, cw, cbv, w_down, h2T, hmid, aT):
    P = C.P
    ones, Bones = C.sb([128, 128], BF16, "ones")
    n2, Bn2 = C.sb([128, KC], F32, "n2")
    cws, Bcws = C.sb([128, 2 * NJ, 3], F32, "cws")
    cbs, Bcbs = C.sb([128, 2 * NJ], F32, "cbs")
    arena, BX = C.sb([128, NJ * 512], BF16, "arena")
    X = arena[:, 0:KC * NTH].rearrange("p (k t) -> p k t", t=NTH)
    AH = arena[:, 0:NJ * 512].rearrange("p (k t) -> p k t", t=512)
    NW = 2 if C.arena else 3
    wraw = [C.sb([128, 43 * 256], BF16, f"wr{i}") for i in range(NW)]
    mA = [C.ps([128, NT], F32, f"mA{i}") for i in range(2)]
    hb, _ = C.ps([128, 512], F32, "hb")
    Bhb = [Buf(f"hb{i}") for i in range(4)]
    hq, Bssqh = C.ps([128, 512], F32, "hq")
    ssq, Bssq = C.ps([128, NT], F32, "ssq")
    rstd, Brstd = C.sb([128, NTH], F32, "rstd")
    stg = [C.sb([128, NTH], F32, f"stg{i}") for i in range(2)]
    sqb = [C.sb([128, NTH], BF16, f"sqb{i}") for i in range(2)]
    uS = [C.sb([128, NTH], F32, f"uS{i}") for i in range(2)]
    uc, Buc = C.sb([128, NT], F32, "uc")
    tm, Btm = C.sb([128, NT], F32, "tm")
    sg, Bsg = C.sb([128, NT], F32, "sg")
    aB = [C.sb([128, NT], BF16, f"aB{i}") for i in range(2)]
    Bhm = [Buf(f"hm{i}") for i in range(KC)]
    BaT = [Buf(f"aT{i}") for i in range(NJ)]

    P.op("dve", lambda e: e.memset(ones[:], 1.0), writes=[Bones])
    P.dma("sp", n2[:], n2w[:, :], writes=[Bn2])
    P.dma("sp", cws[:], cw[:, :, :], writes=[Bcws])
    P.dma("sp", cbs[:], cbv[:, :], writes=[Bcbs])
    for kc in range(KC):
        P.dma("sp" if kc % 2 == 0 else "act", X[:, kc, :], mTh[kc * 128:(kc + 1) * 128, :], writes=[BX], partial=True)

    segs = ((0, 2, None), (2, 514, 0), (514, 1026, 1))

    def mm_block(a, Ba, hslot, lhs_fn, nk, rhs):
        for kc in range(nk):
            for (lo, hi, hf) in segs:
                if hf is None:
                    P.op("pe", lambda e, kc=kc, lo=lo, hi=hi: e.matmul(hb[:, 2 * hslot:2 * hslot + 2], lhsT=lhs_fn(kc), rhs=rhs[:, kc, lo:hi],
                                                                     start=(kc == 0), stop=(kc == nk - 1)),
                         reads=[BX, Bw_cur[0]], writes=[Bhb[hslot]], partial=True)
                else:
                    P.op("pe", lambda e, kc=kc, lo=lo, hi=hi, hf=hf: e.matmul(a[:, hf * 512:(hf + 1) * 512], lhsT=lhs_fn(kc), rhs=rhs[:, kc, lo:hi],
                                                                            start=(kc == 0), stop=(kc == nk - 1)),
                         reads=[BX, Bw_cur[0]], writes=[Ba], partial=True)

    Bw_cur = [None]
    wv = w_out.rearrange("(kc p) c -> p kc c", p=128)
    nblk = 0
    for g in range(D // 256):
        wr, Bw = wraw[g % NW]
        w_ = wr[:, 0:KC * 256].rearrange("p (k c) -> p k c", c=256)
        for q4 in range(4):
            C.wload(w_[:, q4 * 8:(q4 + 1) * 8, :], wv[:, q4 * 8:(q4 + 1) * 8, g * 256:(g + 1) * 256], 8, 256, Bw, "act" if g % 2 else "dve")
        Bw_cur[0] = Bw
        for cb in range(2):
            n = 2 * g + cb
            a, Ba = mA[nblk % 2]
            hs = nblk % 4
            nblk += 1
            r_, Br = stg[n % 2]
            s_, Bs = sqb[n % 2]
            P.dma("sp", r_[:], hTh[n * 128:(n + 1) * 128, :], writes=[Br])
            mm_block(a, Ba, hs, lambda kc, w_=w_, cb=cb: w_[:, kc, cb * 128:(cb + 1) * 128], KC, X)
            P.op("dve", lambda e, a=a, r_=r_: e.tensor_tensor(out=r_[:, 2:NTH], in0=a[:], in1=r_[:, 2:NTH], op=ALU.add), reads=[Ba, Br], writes=[Br])
            P.op("dve", lambda e, r_=r_, hs=hs: e.tensor_tensor(out=r_[:, 0:2], in0=hb[:, 2 * hs:2 * hs + 2], in1=r_[:, 0:2], op=ALU.add),
                 reads=[Bhb[hs], Br], writes=[Br], partial=True)
            P.dma("sp", hmid[n * 128:(n + 1) * 128, :], r_[:], reads=[Br], writes=[Bhm[n]])
            P.op("act", lambda e, r_=r_, s_=s_: e.activation(out=s_[:], in_=r_[:], func=AF.Square), reads=[Br], writes=[Bs])
            for (lo, hi, hf) in segs:
                if hf is None:
                    P.op("pe", lambda e, s_=s_, n=n: e.matmul(hq[:, 0:2], lhsT=ones[:], rhs=s_[:, 0:2], start=(n == 0), stop=(n == KC - 1)),
                         reads=[Bones, Bs], writes=[Bssqh], partial=True)
                else:
                    P.op("pe", lambda e, s_=s_, n=n, lo=lo, hi=hi, hf=hf: e.matmul(ssq[:, hf * 512:(hf + 1) * 512], lhsT=ones[:], rhs=s_[:, lo:hi],
                                                                                 start=(n == 0), stop=(n == KC - 1)),
                         reads=[Bones, Bs], writes=[Bssq], partial=True)
    P.op("act", lambda e: e.activation(out=rstd[:, 2:NTH], in_=ssq[:], func=AF.Ln, scale=1.0 / D, bias=EPS), reads=[Bssq], writes=[Brstd])
    P.op("act", lambda e: e.activation(out=rstd[:, 0:2], in_=hq[:, 0:2], func=AF.Ln, scale=1.0 / D, bias=EPS), reads=[Bssqh], writes=[Brstd], partial=True)
    P.op("act", lambda e: e.activation(out=rstd[:], in_=rstd[:], func=AF.Exp, scale=-0.5), reads=[Brstd], writes=[Brstd])
    for kc in range(KC):
        r_, Br = stg[kc % 2]
        P.dma("sp" if kc % 2 == 0 else "act", r_[:], hmid[kc * 128:(kc + 1) * 128, :], reads=[Bhm[kc]], writes=[Br])
        P.op("dve", lambda e, r_=r_, kc=kc: e.scalar_tensor_tensor(out=X[:, kc, :], in0=r_[:], scalar=n2[:, kc:kc + 1], in1=rstd[:], op0=ALU.mult, op1=ALU.mult),
             reads=[Br, Bn2, Brstd], writes=[BX], partial=(kc > 0))
    uv = w_up.rearrange("(kc p) c -> p kc c", p=128)
    for j in range(NJ):
        wr, Bw = wraw[j % NW]
        w_ = wr[:, 0:KC * 256].rearrange("p (k c) -> p k c", c=256)
        for ub in range(2):
            c0 = ub * DFF + j * 128
            for q4 in range(4):
                C.wload(w_[:, q4 * 8:(q4 + 1) * 8, ub * 128:(ub + 1) * 128], uv[:, q4 * 8:(q4 + 1) * 8, c0:c0 + 128], 8, 128, Bw, "act" if j % 2 else "dve")
        Bw_cur[0] = Bw
        for ub in range(2):
            ch = ub * NJ + j
            a, Ba = mA[nblk % 2]
            hs = nblk % 4
            nblk += 1
            u_, Bu = uS[ub]
            mm_block(a, Ba, hs, lambda kc, w_=w_, ub=ub: w_[:, kc, ub * 128:(ub + 1) * 128], KC, X)
            P.op("act", lambda e, a=a, u_=u_: e.copy(out=u_[:, 2:NTH], in_=a[:]), reads=[Ba], writes=[Bu])
            P.op("dve", lambda e, u_=u_, hs=hs: e.tensor_copy(out=u_[:, 0:2], in_=hb[:, 2 * hs:2 * hs + 2]), reads=[Bhb[hs]], writes=[Bu], partial=True)
            P.op("dve", lambda e, u_=u_, ch=ch: e.tensor_scalar(out=uc[:], in0=u_[:, 2:NTH], scalar1=cws[:, ch, 2:3], scalar2=cbs[:, ch:ch + 1],
                                                                 op0=ALU.mult, op1=ALU.add), reads=[Bu, Bcws, Bcbs], writes=[Buc])
            P.op("dve", lambda e, u_=u_, ch=ch: e.scalar_tensor_tensor(out=uc[:], in0=u_[:, 1:NTH - 1], scalar=cws[:, ch, 1:2], in1=uc[:],
                                                                        op0=ALU.mult, op1=ALU.add), reads=[Bu, Bcws, Buc], writes=[Buc])
            P.op("dve", lambda e, u_=u_, ch=ch: e.scalar_tensor_tensor(out=uc[:], in0=u_[:, 0:NT], scalar=cws[:, ch, 0:1], in1=uc[:],
                                                                        op0=ALU.mult, op1=ALU.add), reads=[Bu, Bcws, Buc], writes=[Buc])
            if ub == 0:
                P.op("act", lambda e: e.activation(out=tm[:], in_=uc[:], func=AF.Exp, scale=-1.0), reads=[Buc], writes=[Btm])
                P.op("dve", lambda e: e.tensor_scalar_add(out=tm[:], in0=tm[:], scalar1=1.0), reads=[Btm], writes=[Btm])
                P.op("dve", lambda e: e.reciprocal(out=tm[:], in_=tm[:]), reads=[Btm], writes=[Btm])
                P.op("dve", lambda e: e.tensor_tensor(out=sg[:], in0=uc[:], in1=tm[:], op=ALU.mult), reads=[Buc, Btm], writes=[Bsg])
            else:
                ab, Bab = aB[j % 2]
                P.op("dve", lambda e, ab=ab: e.tensor_tensor(out=ab[:], in0=uc[:], in1=sg[:], op=ALU.mult), reads=[Buc, Bsg], writes=[Bab])
                P.dma("sp", aT[j], ab[:], reads=[Bab], writes=[BaT[j]])
    dv = w_down.rearrange("(kc p) c -> p kc c", p=128)
    for half in range(2):
        for j0 in range(0, NJ, 8):
            j1 = min(NJ, j0 + 8)
            P.dma("sp", AH[:, j0:j1, :], aT[j0:j1, :, half * 512:(half + 1) * 512].rearrange("j p t -> p j t"),
                  reads=BaT[j0:j1], writes=[BX], partial=(j0 > 0))
        for g in range(D // 256):
            a, Ba = mA[g % 2]
            for kh in range(2):
                wr, Bw = wraw[(2 * g + kh) % NW]
                w_ = wr[:, 0:43 * 256].rearrange("p (k c) -> p k c", c=256)
                for (k0, k1) in ((0, 8), (8, 16), (16, 24), (24, 32), (32, 40), (40, 43)):
                    C.wload(w_[:, k0:k1, :], dv[:, kh * 43 + k0:kh * 43 + k1, g * 256:(g + 1) * 256], k1 - k0, 256, Bw, "act" if (2 * g + kh) % 2 else "dve")
                for cb in range(2):
                    for kc in range(43):
                        kk = kh * 43 + kc
                        P.op("pe", lambda e, a=a, w_=w_, kc=kc, kk=kk, cb=cb: e.matmul(a[:, cb * 512:(cb + 1) * 512], lhsT=w_[:, kc, cb * 128:(cb + 1) * 128],
                                                                                 rhs=AH[:, kk, :], start=(kk == 0), stop=(kk == NJ - 1)),
                             reads=[BX, Bw], writes=[Ba], partial=True)
            for cb in range(2):
                n = 2 * g + cb
                r_, Br = stg[n % 2]
                P.dma("act", r_[:, 0:512], hmid[n * 128:(n + 1) * 128, 2 + half * 512:2 + (half + 1) * 512], reads=[Bhm[n]], writes=[Br])
                P.op("dve", lambda e, a=a, r_=r_, cb=cb: e.tensor_tensor(out=r_[:, 0:512], in0=a[:, cb * 512:(cb + 1) * 512], in1=r_[:, 0:512], op=ALU.add),
                     reads=[Ba, Br], writes=[Br])
                P.dma("sp", h2T[n * 128:(n + 1) * 128, half * 512:(half + 1) * 512], r_[:, 0:512], reads=[Br])


def build_D():
    nc = bass.Bass("TRN2", target_bir_lowering=False)
    hT = nc.dram_tensor("hT", [D, NT], F32, kind="ExternalInput").ap()
    fw = nc.dram_tensor("fw", [128, KC], F32, kind="ExternalInput").ap()
    oT = nc.dram_tensor("oT", [D, NT], F32, kind="ExternalOutput").ap()
    C = Ctx(nc)
    emit_D(C, hT, fw, oT)
    C.finish()
    return nc


def emit_D(C, hT, fw, oT):
    P = C.P
    ones, Bones = C.sb([128, 128], BF16, "ones")
    n1, Bn1 = C.sb([128, KC], F32, "n1")
    rstd, Brstd = C.sb([128, NT], F32, "rstd")
    t0, Bt0 = C.sb([128, NT], F32, "t0")
    ch = [C.sb([128, NT], F32, f"ch{i}") for i in range(3)]
    sq = [C.sb([128, NT], BF16, f"sq{i}") for i in range(2)]
    ob = [C.sb([128, NT], F32, f"ob{i}") for i in range(2)]
    ssq, Bssq = C.ps([128, NT], F32, "ssq")
    P.op("dve", lambda e: e.memset(ones[:], 1.0), writes=[Bones])
    P.dma("sp", n1[:], fw[:, :], writes=[Bn1])
    for kc in range(KC):
        c, Bc = ch[kc % 3]
        s, Bs = sq[kc % 2]
        P.dma("sp" if kc % 2 == 0 else "act", c[:], hT[kc * 128:(kc + 1) * 128, :], writes=[Bc])
        P.op("act", lambda e, c=c, s=s: e.activation(out=s[:], in_=c[:], func=AF.Square), reads=[Bc], writes=[Bs])
        for hf in range(2):
            P.op("pe", lambda e, s=s, hf=hf, kc=kc: e.matmul(ssq[:, hf * 512:(hf + 1) * 512], lhsT=ones[:], rhs=s[:, hf * 512:(hf + 1) * 512],
                                                          start=(kc == 0), stop=(kc == KC - 1)),
                 reads=[Bs, Bones], writes=[Bssq], partial=True)
    P.op("act", lambda e: e.activation(out=t0[:], in_=ssq[:], func=AF.Ln, scale=1.0 / D, bias=EPS), reads=[Bssq], writes=[Bt0])
    P.op("act", lambda e: e.activation(out=rstd[:], in_=t0[:], func=AF.Exp, scale=-0.5), reads=[Bt0], writes=[Brstd])
    for kc in range(KC):
        c, Bc = ch[kc % 3]
        o_, Bo = ob[kc % 2]
        P.dma("sp" if kc % 2 == 0 else "act", c[:], hT[kc * 128:(kc + 1) * 128, :], writes=[Bc])
        P.op("dve", lambda e, c=c, o_=o_, kc=kc: e.scalar_tensor_tensor(out=o_[:], in0=c[:], scalar=n1[:, kc:kc + 1], in1=rstd[:], op0=ALU.mult, op1=ALU.mult),
             reads=[Bc, Bn1, Brstd], writes=[Bo])
        P.dma("sp", oT[kc * 128:(kc + 1) * 128, :], o_[:], reads=[Bo])


TP = T + 2
NTILE = T // NT


def build_L():
    nc = bass.Bass("TRN2", target_bir_lowering=False)
    ds = bass.ds
    hpad = nc.dram_tensor("hpad", [D, TP], F32, kind="ExternalInput").ap()
    w_in = nc.dram_tensor("w_in", [D, INC], F32, kind="ExternalInput").ap()
    n1w = nc.dram_tensor("n1w", [128, KC], F32, kind="ExternalInput").ap()
    lbp = nc.dram_tensor("lbp", [128, 16, DEPTH], F32, kind="ExternalInput").ap()
    lsel = nc.dram_tensor("lsel", [128, 16, DEPTH], F32, kind="ExternalInput").ap()
    nwa = nc.dram_tensor("nwa", [NCORE, 128, 4], F32, kind="ExternalInput").ap()
    cst = _const_aps(nc)
    zpad = nc.dram_tensor("zpad", [D, 2], BF16, kind="ExternalInput").ap()
    w_out = nc.dram_tensor("w_out", [D, D], F32, kind="ExternalInput").ap()
    n2w = nc.dram_tensor("n2w", [128, KC], F32, kind="ExternalInput").ap()
    w_up = nc.dram_tensor("w_up", [D, 2 * DFF], F32, kind="ExternalInput").ap()
    cw = nc.dram_tensor("cw", [128, 2 * NJ, 3], F32, kind="ExternalInput").ap()
    cbv = nc.dram_tensor("cb", [128, 2 * NJ], F32, kind="ExternalInput").ap()
    w_down = nc.dram_tensor("w_down", [DFF, D], F32, kind="ExternalInput").ap()
    h2T = nc.dram_tensor("h2T", [D, T], F32, kind="ExternalOutput").ap()
    qkT = nc.dram_tensor("s_qkT", [32 * 128, T], BF16).ap()
    vtok = nc.dram_tensor("s_vtok", [T, SBW], BF16).ap()
    hgT = nc.dram_tensor("s_hgT", [3 * 16 * 128, T], F32).ap()
    itok = nc.dram_tensor("s_itok", [T, HGW], BF16).ap()
    ggT = nc.dram_tensor("s_ggT", [16 * 128, T], F32).ap()
    mTp = nc.dram_tensor("s_mTp", [D, TP], BF16).ap()
    a_h = nc.dram_tensor("a_h", [D, NT], F32).ap()
    a_qk = nc.dram_tensor("a_qk", [32, 128, NT], BF16).ap()
    a_v = nc.dram_tensor("a_v", [NT, SBW], BF16).ap()
    a_hg = nc.dram_tensor("a_hg", [3, 16, 128, NT], F32).ap()
    a_i = nc.dram_tensor("a_i", [NT, HGW], BF16).ap()
    a_gg = nc.dram_tensor("a_gg", [16, 128, NT], F32).ap()
    b_q = nc.dram_tensor("b_q", [2, 128, T], BF16).ap()
    b_k = nc.dram_tensor("b_k", [2, 128, T], BF16).ap()
    b_v = nc.dram_tensor("b_v", [T, 256], BF16).ap()
    b_hq = nc.dram_tensor("b_hq", [2, 128, T], F32).ap()
    b_hk = nc.dram_tensor("b_hk", [2, 128, T], F32).ap()
    b_hg = nc.dram_tensor("b_hg", [2, 128, T], F32).ap()
    b_i = nc.dram_tensor("b_i", [T, 256], BF16).ap()
    b_gg = nc.dram_tensor("b_gg", [2, 128, T], F32).ap()
    b_nw = nc.dram_tensor("b_nw", [128, 4], F32).ap()
    b_m = nc.dram_tensor("b_m", [4, 128, T], BF16).ap()
    c_m = nc.dram_tensor("c_m", [D, NTH], BF16).ap()
    c_h = nc.dram_tensor("c_h", [D, NTH], F32).ap()
    c_o = nc.dram_tensor("c_o", [D, NT], F32).ap()
    hmid = nc.dram_tensor("s_hmid", [D, NTH], F32).ap()
    aT = nc.dram_tensor("s_aT", [NJ, 128, NT], BF16).ap()

    C = Ctx(nc, arena=True)
    C.begin()
    C.P.dma("sp", mTp[:, 0:2], zpad[:, :])
    C.end_body()
    fl3 = lambda ap: ap.rearrange("h p t -> (h p) t")
    with nc.Fori(0, NTILE) as i:
        C.begin()
        P = C.P
        P.dma("sp", a_h[:, :], hpad[:, ds(i * NT + 2, NT)])
        P.barrier()
        emit_A(C, a_h, w_in, n1w, lbp, lsel, a_qk, a_v, a_hg, a_i, a_gg)
        P.barrier()
        tsl = ds(i * NT, NT)
        P.dma("sp", qkT[:, tsl], fl3(a_qk))
        P.dma("act", vtok[tsl, :], a_v[:, :])
        P.dma("sp", hgT[:, tsl], a_hg.rearrange("a h p t -> (a h p) t"))
        P.dma("act", itok[tsl, :], a_i[:, :])
        P.dma("act", ggT[:, tsl], fl3(a_gg))
        C.end_body()
    with nc.Fori(0, NCORE) as i:
        C.begin()
        P = C.P
        g256 = ds(i * 256, 256)
        P.dma("sp", fl3(b_q), qkT[g256, :])
        P.dma("act", fl3(b_k), qkT[ds(i * 256 + 2048, 256), :])
        P.dma("sp", b_v[0:T // 2, :], vtok[0:T // 2, g256])
        P.dma("act", b_v[T // 2:T, :], vtok[T // 2:T, g256])
        P.dma("sp", fl3(b_hq), hgT[g256, :])
        P.dma("act", fl3(b_hk), hgT[ds(i * 256 + 2048, 256), :])
        P.dma("sp", fl3(b_hg), hgT[ds(i * 256 + 4096, 256), :])
        P.dma("act", b_i[0:T // 2, :], itok[0:T // 2, g256])
        P.dma("sp", b_i[T // 2:T, :], itok[T // 2:T, g256])
        P.dma("act", fl3(b_gg), ggT[g256, :])
        P.dma("sp", b_nw[:, :], nwa[i])
        P.barrier()
        emit_B(C, T, b_q, b_k, b_v, b_hq, b_hk, b_hg, b_i, b_gg, b_nw, cst, b_m[0:2], b_m[2:4])
        P.barrier()
        P.dma("sp", mTp[g256, 2:TP], fl3(b_m[0:2]))
        P.dma("act", mTp[ds(i * 256 + SBW, 256), 2:TP], fl3(b_m[2:4]))
        C.end_body()
    with nc.Fori(0, NTILE) as i:
        C.begin()
        P = C.P
        P.dma("sp", c_m[:, :], mTp[:, ds(i * NT, NTH)])
        P.dma("act", c_h[:, :], hpad[:, ds(i * NT, NTH)])
        P.barrier()
        emit_C(C, c_m, c_h, w_out, n2w, w_up, cw, cbv, w_down, c_o, hmid, aT)
        P.barrier()
        P.dma("sp", h2T[:, ds(i * NT, NT)], c_o[:, :])
        C.end_body()
    C.st.close()
    return nc


_PROGS = {}


def _prog(name):
    if name not in _PROGS:
        _PROGS[name] = {"A": build_A, "B": build_B, "C": build_C, "D": build_D, "L": build_L}[name]()
    return _PROGS[name]


def _fm(v):
    return np.ascontiguousarray(np.asarray(v, np.float32).reshape(-1, 128).T)


def layer_inputs(l, hT, norm1_w, w_in, sb_norm_w, hg_lb_param, hg_norm_w, w_out, norm2_w, w_up, conv_w, conv_b, w_down):
    sel = np.zeros(DEPTH, np.float32)
    sel[1:l + 1] = 1.0
    sbw = np.asarray(sb_norm_w[l], np.float32).reshape(16, 128)
    hgw = np.asarray(hg_norm_w[l], np.float32).reshape(16, 128)
    nwa = np.stack([np.stack([sbw[2 * c], sbw[2 * c + 1], hgw[2 * c], hgw[2 * c + 1]], axis=1) for c in range(NCORE)], axis=0)
    d = {"hpad": np.ascontiguousarray(np.concatenate([np.zeros((D, 2), np.float32), hT], axis=1)),
         "w_in": np.ascontiguousarray(np.asarray(w_in[l], np.float32)),
         "n1w": _fm(norm1_w[l]),
         "lbp": np.ascontiguousarray(np.asarray(hg_lb_param, np.float32).reshape(DEPTH, 16, 128).transpose(2, 1, 0)),
         "lsel": np.ascontiguousarray(np.broadcast_to(sel, (128, 16, DEPTH))),
         "nwa": np.ascontiguousarray(nwa.astype(np.float32)),
         "zpad": np.zeros((D, 2), ml_dtypes.bfloat16),
         "w_out": np.ascontiguousarray(np.asarray(w_out[l], np.float32)),
         "n2w": _fm(norm2_w[l]),
         "w_up": np.ascontiguousarray(np.asarray(w_up[l], np.float32)),
         "cw": np.ascontiguousarray(np.asarray(conv_w[l], np.float32).reshape(3, 2 * NJ, 128).transpose(2, 1, 0)),
         "cb": np.ascontiguousarray(np.asarray(conv_b[l], np.float32).reshape(2 * NJ, 128).T),
         "w_down": np.ascontiguousarray(np.asarray(w_down[l], np.float32))}
    d.update(mixer_consts())
    return d


def layer_inputs(l, hT, norm1_w, w_in, sb_norm_w, hg_lb_param, hg_norm_w, w_out, norm2_w, w_up, conv_w, conv_b, w_down):
    sel = np.zeros(DEPTH, np.float32)
    sel[1:l + 1] = 1.0
    sbw = np.asarray(sb_norm_w[l], np.float32).reshape(16, 128)
    hgw = np.asarray(hg_norm_w[l], np.float32).reshape(16, 128)
    nwa = np.stack([np.stack([sbw[2 * c], sbw[2 * c + 1], hgw[2 * c], hgw[2 * c + 1]], axis=1) for c in range(NCORE)], axis=0)
    d = {"hpad": np.ascontiguousarray(np.concatenate([np.zeros((D, 2), np.float32), hT], axis=1)),
         "w_in": np.ascontiguousarray(np.asarray(w_in[l], np.float32)),
         "n1w": _fm(norm1_w[l]),
         "lbp": np.ascontiguousarray(np.asarray(hg_lb_param, np.float32).reshape(DEPTH, 16, 128).transpose(2, 1, 0)),
         "lsel": np.ascontiguousarray(np.broadcast_to(sel, (128, 16, DEPTH))),
         "nwa": np.ascontiguousarray(nwa.astype(np.float32)),
         "zpad": np.zeros((D, 2), ml_dtypes.bfloat16),
         "w_out": np.ascontiguousarray(np.asarray(w_out[l], np.float32)),
         "n2w": _fm(norm2_w[l]),
         "w_up": np.ascontiguousarray(np.asarray(w_up[l], np.float32)),
         "cw": np.ascontiguousarray(np.asarray(conv_w[l], np.float32).reshape(3, 2 * NJ, 128).transpose(2, 1, 0)),
         "cb": np.ascontiguousarray(np.asarray(conv_b[l], np.float32).reshape(2 * NJ, 128).T),
         "w_down": np.ascontiguousarray(np.asarray(w_down[l], np.float32))}
    d.update(mixer_consts())
    return d


def kernel(x, norm1_w, w_in, sb_norm_w, hg_lb_param, hg_norm_w, w_out, norm2_w, w_up, conv_w, conv_b,
           w_down, final_norm_w):
    x = np.asarray(x, np.float32)
    hT = np.ascontiguousarray(x[0].T)
    for l in range(DEPTH):
        ins = layer_inputs(l, hT, norm1_w, w_in, sb_norm_w, hg_lb_param, hg_norm_w, w_out, norm2_w, w_up, conv_w, conv_b, w_down)
        r = run_bass_kernel_spmd(_prog("L"), [ins], core_ids=[0]).results
        hT = r[0]["h2T"]
        del ins, r
    cores = list(range(NCORE))
    fw = _fm(final_norm_w)
    ins = [{"hT": np.ascontiguousarray(hT[:, c * NT:(c + 1) * NT]), "fw": fw} for c in cores]
    rd = run_bass_kernel_spmd(_prog("D"), ins, core_ids=cores).results
    oT = np.concatenate([r["oT"] for r in rd], axis=1)
    return np.ascontiguousarray(oT.T)[None].astype(np.float32)
```

```python
import contextlib
import numpy as np
import ml_dtypes
import concourse.bass as bass
import concourse.mybir as mybir
from concourse.bass_utils import run_bass_kernel_spmd

F32 = mybir.dt.float32
BF16 = mybir.dt.bfloat16
AF = mybir.ActivationFunctionType
ALU = mybir.AluOpType
AX = mybir.AxisListType

D = 4096
T = 8192
DEPTH = 4
NCORE = 8
NT = T // NCORE
KC = D // 128
SBW = 2048
HGW = 2048
INC = 14336
DFF = 11008
EPS = 1e-6

ENGS = ("pe", "act", "dve", "pool", "sp")


class Buf:
    __slots__ = ("name", "writers", "readers", "gen_deps")

    def __init__(self, name):
        self.name = name
        self.writers = []
        self.readers = []
        self.gen_deps = []


class Op:
    __slots__ = ("eng", "fn", "deps", "is_dma", "sem", "val", "has_dep", "idx")

    def __init__(self, eng, fn, is_dma):
        self.eng = eng
        self.fn = fn
        self.deps = []
        self.is_dma = is_dma
        self.sem = None
        self.val = 0
        self.has_dep = False
        self.idx = None


class Prog:
    def __init__(self, nc):
        self.nc = nc
        self.ops = {e: [] for e in ENGS}
        self.all_ops = []
        self.dma_ops = []
        self.last = {e: None for e in ENGS}
        self.dma_since_bar = []

    def op(self, eng, fn, reads=(), writes=(), partial=False, dma=False, extra_deps=()):
        o = Op(eng, fn, dma)
        deps = list(extra_deps)
        for b in reads:
            deps.extend(b.writers)
        for b in writes:
            if b.readers:
                b.gen_deps = list(b.readers) + list(b.writers)
                deps.extend(b.gen_deps)
                b.writers = []
                b.readers = []
            elif not partial:
                b.gen_deps = list(b.writers)
                deps.extend(b.gen_deps)
                b.writers = []
        for b in reads:
            if dma:
                b.readers.append(o)
            else:
                b.readers = [r for r in b.readers if r.is_dma or r.eng != eng] + [o]
        for b in writes:
            if dma:
                b.writers.append(o)
            else:
                b.writers = [w for w in b.writers if w.is_dma or w.eng != eng] + [o]
        seen = set()
        for d in deps:
            if d is None or id(d) in seen or d is o:
                continue
            seen.add(id(d))
            if d.eng == "pe" and eng == "pe" and not d.is_dma and not dma:
                continue
            o.deps.append(d)
            d.has_dep = True
        self.ops[eng].append(o)
        self.all_ops.append(o)
        if dma:
            self.dma_ops.append(o)
            self.dma_since_bar.append(o)
        else:
            self.last[eng] = o
        return o

    def dma(self, eng, out, in_, reads=(), writes=(), partial=False, **kw):
        return self.op(eng, lambda e: e.dma_start(out=out, in_=in_, **kw), reads, writes,
                       partial=partial, dma=True)

    def barrier(self):
        lasts = [self.last[e] for e in ENGS if self.last[e] is not None]
        dmas = list(self.dma_since_bar)
        self.dma_since_bar = []
        for e in ENGS:
            self.op(e, None, extra_deps=lasts + dmas)

    def emit(self, final_wait_eng="sp", loop=False):
        nc = self.nc
        with contextlib.ExitStack() as st:
            esem = {e: st.enter_context(nc.semaphore("s_" + e)) for e in ENGS}
            ecount = {e: 0 for e in ENGS}
            NPOOL = 12
            dpool = {q: [st.enter_context(nc.semaphore(f"d_{q}{i}")) for i in range(NPOOL)]
                     for q in ("sp", "act", "pool")}
            dcount = {q: [0] * NPOOL for q in ("sp", "act", "pool")}
            dnext = {q: 0 for q in ("sp", "act", "pool")}
            for o in self.all_ops:
                if o.is_dma:
                    q = o.eng
                    k = dnext[q] % NPOOL
                    dnext[q] += 1
                    o.sem = dpool[q][k]
                    prev = dcount[q][k]
                    dcount[q][k] += 16
                    o.val = dcount[q][k]
                    o.idx = prev
                elif o.has_dep and o.fn is not None:
                    ecount[o.eng] += 1
                    o.sem = esem[o.eng]
                    o.val = ecount[o.eng]
                elif o.has_dep:
                    o.sem = esem[o.eng]
                    o.val = ecount[o.eng]
            final_waits = [(o.sem, o.val) for o in self.dma_ops]

            with nc.Block() as block:
                def run(engname):
                    def body(eng):
                        waited = {}

                        def w(s, v):
                            if v <= 0 or waited.get(id(s), 0) >= v:
                                return
                            waited[id(s)] = v
                            eng.wait_ge(s, v)
                        for o in self.ops[engname]:
                            if o.is_dma:
                                w(o.sem, o.idx)
                            if len(o.deps) > 4:
                                mx = {}
                                for d in o.deps:
                                    k = id(d.sem)
                                    if k not in mx or mx[k][1] < d.val:
                                        mx[k] = (d.sem, d.val)
                                for (s_, v_) in mx.values():
                                    w(s_, v_)
                            else:
                                for d in o.deps:
                                    w(d.sem, d.val)
                            if o.fn is None:
                                continue
                            ins = o.fn(eng)
                            if o.is_dma:
                                ins.then_inc(o.sem, 16)
                            elif o.has_dep:
                                ins.then_inc(o.sem, 1)
                        if engname == final_wait_eng:
                            for (s, v) in final_waits:
                                w(s, v)
                    return body
                block.tensor(run("pe"))
                block.scalar(run("act"))
                block.vector(run("dve"))
                if self.ops["pool"]:
                    block.gpsimd(run("pool"))
                block.sync(run("sp"))
            if loop:
                nc.all_engine_barrier()
                for sm in list(esem.values()) + [x_ for q in dpool.values() for x_ in q]:
                    nc.sync.sem_clear(sm)
                nc.all_engine_barrier()


class Ctx:
    AW = 51200

    def __init__(self, nc, arena=False):
        self.nc = nc
        self.st = contextlib.ExitStack()
        self.P = Prog(nc)
        self.n = 0
        self.arena = arena
        self.use_stage = arena
        self.wst = None
        self.nwl = 0
        self.pool_eng = "dve" if arena else "pool"
        if arena:
            self.ar = self.st.enter_context(nc.sbuf_tensor("arena", [128, self.AW], F32))
            self.pp = self.st.enter_context(nc.psum_tensor("parena", [128, 4096], F32))
            self.off = 0
            self.poff = 0

    def phase(self):
        self.P.barrier()
        self.off = 0
        self.poff = 0

    def sb(self, shape, dt, name=None):
        self.n += 1
        nm = name or f"t{self.n}"
        if not self.arena:
            t = self.st.enter_context(self.nc.sbuf_tensor(nm, list(shape), dt))
            return t, Buf(nm)
        nel = int(np.prod(shape[1:]))
        words = nel if dt == F32 else (nel + 1) // 2
        assert self.off + words <= self.AW, (nm, self.off, words)
        v = self.ar[0:shape[0], self.off:self.off + words]
        self.off += words
        if dt != F32:
            v = v.bitcast(dt)[:, 0:nel]
        if len(shape) == 3:
            v = v.rearrange("p (a b) -> p a b", b=shape[2])
        return v, Buf(nm)

    def ps(self, shape, dt=F32, name=None):
        self.n += 1
        nm = name or f"p{self.n}"
        if not self.arena:
            t = self.st.enter_context(self.nc.psum_tensor(nm, list(shape), dt))
            return t, Buf(nm)
        words = ((shape[1] + 511) // 512) * 512
        assert self.poff + words <= 4096, (nm, self.poff)
        v = self.pp[:, self.poff:self.poff + shape[1]]
        self.poff += words
        return v, Buf(nm)

    def begin(self):
        self.P = Prog(self.nc)
        self.off = 0
        self.poff = 0
        self.wst = None
        self.nwl = 0

    def end_body(self):
        self.P.emit(loop=True)

    def wload(self, dst, src, k, c, Bw, eng):
        P = self.P
        if not self.use_stage:
            P.dma("pool", dst, src, writes=[Bw], partial=True)
            return
        if self.wst is None:
            nst = max(2, min(6, (self.AW - self.off) // 2048))
            self.wst = [self.sb([128, 2048], F32, f"wst{i}") for i in range(nst)]
        st, Bst = self.wst[self.nwl % len(self.wst)]
        q = "sp" if self.nwl % 2 == 0 else "act"
        self.nwl += 1
        v = st[:, 0:k * c].rearrange("p (k c) -> p k c", c=c)
        P.dma(q, v, src, writes=[Bst])
        if eng == "act":
            P.op("act", lambda e: e.copy(out=dst, in_=v), reads=[Bst], writes=[Bw], partial=True)
        else:
            P.op("dve", lambda e: e.tensor_copy(out=dst, in_=v), reads=[Bst], writes=[Bw], partial=True)

    def finish(self):
        self.P.emit()
        self.st.close()


def _bf(a):
    return np.ascontiguousarray(a).astype(ml_dtypes.bfloat16)


def build_A():
    nc = bass.Bass("TRN2", target_bir_lowering=False)
    hT = nc.dram_tensor("hT", [D, NT], F32, kind="ExternalInput").ap()
    w_in = nc.dram_tensor("w_in", [D, INC], F32, kind="ExternalInput").ap()
    n1w = nc.dram_tensor("n1w", [128, KC], F32, kind="ExternalInput").ap()
    lbp = nc.dram_tensor("lbp", [128, 16, DEPTH], F32, kind="ExternalInput").ap()
    lsel = nc.dram_tensor("lsel", [128, 16, DEPTH], F32, kind="ExternalInput").ap()
    qkT = nc.dram_tensor("qkT", [32, 128, NT], BF16, kind="ExternalOutput").ap()
    vtok = nc.dram_tensor("vtok", [NT, SBW], BF16, kind="ExternalOutput").ap()
    hgT = nc.dram_tensor("hgT", [3, 16, 128, NT], F32, kind="ExternalOutput").ap()
    itok = nc.dram_tensor("itok", [NT, HGW], BF16, kind="ExternalOutput").ap()
    ggT = nc.dram_tensor("ggT", [16, 128, NT], F32, kind="ExternalOutput").ap()
    C = Ctx(nc)
    emit_A(C, hT, w_in, n1w, lbp, lsel, qkT, vtok, hgT, itok, ggT)
    C.finish()
    return nc


def emit_A(C, hT, w_in, n1w, lbp, lsel, qkT, vtok, hgT, itok, ggT):
    P = C.P
    ones, Bones = C.sb([128, 128], BF16, "ones")
    n1, Bn1 = C.sb([128, KC], F32, "n1")
    lb, Blb = C.sb([128, 16], F32, "lb")
    oml, Boml = C.sb([128, 16], F32, "oml")
    lbe, Blbe = C.sb([128, 16, DEPTH], F32, "lbe")
    lbs, Blbs = C.sb([128, 16, DEPTH], F32, "lbs")
    lt1, Blt1 = C.sb([128, 16], F32, "lt1")
    lt2, Blt2 = C.sb([128, 16], F32, "lt2")
    hn, Bhn = C.sb([128, KC, NT], BF16, "hn")
    rstd, Brstd = C.sb([128, NT], F32, "rstd")
    NCH = 3
    ch = [C.sb([128, NT], F32, f"ch{i}") for i in range(NCH)]
    sq = [C.sb([128, NT], BF16, f"sq{i}") for i in range(2)]
    NW = 3
    wt = [C.sb([128, KC, 256], BF16, f"wt{i}") for i in range(NW)]
    acc = [C.ps([128, NT], F32, f"acc{i}") for i in range(4)]
    tmp = [C.sb([128, NT], F32, f"tmp{i}") for i in range(2)]
    stg = [C.sb([128, NT], F32, f"stg{i}") for i in range(3)]
    stb = [C.sb([128, NT], BF16, f"stb{i}") for i in range(2)]
    stk = [C.sb([128, NT // 128, 256], BF16, f"stk{i}") for i in range(2)]

    P.op("dve", lambda e: e.memset(ones[:], 1.0), writes=[Bones])
    P.dma("sp", n1[:], n1w[:, :], writes=[Bn1])
    P.dma("sp", lbe[:], lbp[:, :, :], writes=[Blbe])
    P.dma("sp", lbs[:], lsel[:, :, :], writes=[Blbs])
    P.op("act", lambda e: e.activation(out=lbe[:], in_=lbe[:], func=AF.Exp), reads=[Blbe], writes=[Blbe])
    P.op("dve", lambda e: e.tensor_reduce(out=lt1[:], in_=lbe[:], axis=AX.X, op=ALU.add), reads=[Blbe], writes=[Blt1])
    P.op("dve", lambda e: e.tensor_tensor(out=lbs[:], in0=lbe[:], in1=lbs[:], op=ALU.mult), reads=[Blbe, Blbs], writes=[Blbs])
    P.op("dve", lambda e: e.tensor_reduce(out=lt2[:], in_=lbs[:], axis=AX.X, op=ALU.add), reads=[Blbs], writes=[Blt2])
    P.op("dve", lambda e: e.reciprocal(out=lt1[:], in_=lt1[:]), reads=[Blt1], writes=[Blt1])
    P.op("dve", lambda e: e.tensor_tensor(out=lb[:], in0=lt2[:], in1=lt1[:], op=ALU.mult), reads=[Blt1, Blt2], writes=[Blb])
    P.op("dve", lambda e: e.tensor_scalar(out=oml[:], in0=lb[:], scalar1=-1.0, scalar2=1.0, op0=ALU.mult, op1=ALU.add),
         reads=[Blb], writes=[Boml])

    ssq, Bssq = acc[0]
    for kc in range(KC):
        c, Bc = ch[kc % NCH]
        s, Bs = sq[kc % 2]
        P.dma("sp" if kc % 2 == 0 else "act", c[:], hT[kc * 128:(kc + 1) * 128, :], writes=[Bc])
        P.op("act", lambda e, c=c, s=s: e.activation(out=s[:], in_=c[:], func=AF.Square), reads=[Bc], writes=[Bs])
        for hf in range(2):
            P.op("pe", lambda e, s=s, hf=hf, kc=kc: e.matmul(ssq[:, hf * 512:(hf + 1) * 512], lhsT=ones[:], rhs=s[:, hf * 512:(hf + 1) * 512],
                                                          start=(kc == 0), stop=(kc == KC - 1)),
                 reads=[Bs, Bones], writes=[Bssq], partial=True)
    t0, Bt0 = tmp[0]
    P.op("act", lambda e: e.activation(out=t0[:], in_=ssq[:], func=AF.Ln, scale=1.0 / D, bias=EPS), reads=[Bssq], writes=[Bt0])
    P.op("act", lambda e: e.activation(out=rstd[:], in_=t0[:], func=AF.Exp, scale=-0.5), reads=[Bt0], writes=[Brstd])
    for kc in range(KC):
        c, Bc = ch[kc % NCH]
        P.dma("sp" if kc % 2 == 0 else "act", c[:], hT[kc * 128:(kc + 1) * 128, :], writes=[Bc])
        P.op("dve", lambda e, c=c, kc=kc: e.scalar_tensor_tensor(out=hn[:, kc, :], in0=c[:], scalar=n1[:, kc:kc + 1], in1=rstd[:],
                                                                 op0=ALU.mult, op1=ALU.mult),
             reads=[Bc, Bn1, Brstd], writes=[Bhn], partial=True)
    wv = w_in.rearrange("(kc p) c -> p kc c", p=128)
    NG = INC // 256
    na = 0
    nst = 0
    for g in range(NG):
        w_, Bw = wt[g % NW]
        for q4 in range(4):
            C.wload(w_[:, q4 * 8:(q4 + 1) * 8, :], wv[:, q4 * 8:(q4 + 1) * 8, g * 256:(g + 1) * 256], 8, 256, Bw, "act" if g % 2 else "dve")
        col0 = g * 256
        seg = col0 // 2048
        if seg in (2, 5):
            sk, Bsk = stk[nst % 2]
            nst += 1
            for tb in range(NT // 128):
                a, Ba = acc[na % 4]
                na += 1
                for kc in range(KC):
                    P.op("pe", lambda e, a=a, w_=w_, kc=kc, tb=tb: e.matmul(a[:, 0:256], lhsT=hn[:, kc, tb * 128:(tb + 1) * 128], rhs=w_[:, kc, :],
                                                                          start=(kc == 0), stop=(kc == KC - 1)),
                         reads=[Bhn, Bw], writes=[Ba], partial=True)
                if tb % 2 == 0:
                    P.op("act", lambda e, a=a, sk=sk, tb=tb: e.copy(out=sk[:, tb, :], in_=a[:, 0:256]), reads=[Ba], writes=[Bsk], partial=True)
                else:
                    P.op("dve", lambda e, a=a, sk=sk, tb=tb: e.tensor_copy(out=sk[:, tb, :], in_=a[:, 0:256]), reads=[Ba], writes=[Bsk], partial=True)
            dst = vtok if seg == 2 else itok
            c0 = col0 - (4096 if seg == 2 else 10240)
            P.dma("sp", dst[:, c0:c0 + 256].rearrange("(b p) c -> p b c", p=128), sk[:], reads=[Bsk])
            continue
        for cb in range(2):
            a, Ba = acc[na % 4]
            na += 1
            for kc in range(KC):
                for hf in range(2):
                    P.op("pe", lambda e, a=a, w_=w_, kc=kc, hf=hf, cb=cb: e.matmul(a[:, hf * 512:(hf + 1) * 512], lhsT=w_[:, kc, cb * 128:(cb + 1) * 128],
                                                                                 rhs=hn[:, kc, hf * 512:(hf + 1) * 512],
                                                                                 start=(kc == 0), stop=(kc == KC - 1)),
                         reads=[Bhn, Bw], writes=[Ba], partial=True)
            blk = (col0 % 2048) // 128 + cb
            if seg == 0:
                o_, Bo = stb[na % 2]
                P.op("act", lambda e, a=a, o_=o_: e.activation(out=o_[:], in_=a[:], func=AF.Copy, scale=float(128 ** -0.5)), reads=[Ba], writes=[Bo])
                P.dma("sp", qkT[blk], o_[:], reads=[Bo])
            elif seg == 1:
                o_, Bo = stb[na % 2]
                P.op("dve", lambda e, a=a, o_=o_: e.tensor_copy(out=o_[:], in_=a[:]), reads=[Ba], writes=[Bo])
                P.dma("sp", qkT[16 + blk], o_[:], reads=[Bo])
            elif seg in (3, 6):
                t_, Bt = tmp[na % 2]
                o_, Bo = stg[na % 3]
                P.op("act", lambda e, a=a, t_=t_: e.activation(out=t_[:], in_=a[:], func=AF.Exp, scale=-1.0), reads=[Ba], writes=[Bt])
                P.op("dve", lambda e, t_=t_: e.tensor_scalar_add(out=t_[:], in0=t_[:], scalar1=1.0), reads=[Bt], writes=[Bt])
                P.op("dve", lambda e, t_=t_: e.reciprocal(out=t_[:], in_=t_[:]), reads=[Bt], writes=[Bt])
                P.op("dve", lambda e, a=a, t_=t_, o_=o_: e.tensor_tensor(out=o_[:], in0=a[:], in1=t_[:], op=ALU.mult), reads=[Ba, Bt], writes=[Bo])
                P.dma("sp", hgT[0, blk] if seg == 3 else ggT[blk], o_[:], reads=[Bo])
            else:
                t_, Bt = tmp[na % 2]
                o1, Bo1 = stg[0]
                o2, Bo2 = stg[1]
                P.op("act", lambda e, a=a, t_=t_: e.activation(out=t_[:], in_=a[:], func=AF.Exp, scale=-1.0), reads=[Ba], writes=[Bt])
                P.op("dve", lambda e, t_=t_: e.tensor_scalar_add(out=t_[:], in0=t_[:], scalar1=1.0), reads=[Bt], writes=[Bt])
                P.op("dve", lambda e, t_=t_: e.reciprocal(out=t_[:], in_=t_[:]), reads=[Bt], writes=[Bt])
                P.op("dve", lambda e, t_=t_, blk=blk: e.tensor_scalar(out=t_[:], in0=t_[:], scalar1=oml[:, blk:blk + 1], scalar2=lb[:, blk:blk + 1],
                                                                        op0=ALU.mult, op1=ALU.add), reads=[Bt, Boml, Blb], writes=[Bt])
                P.op("act", lambda e, t_=t_, o1=o1: e.activation(out=o1[:], in_=t_[:], func=AF.Ln), reads=[Bt], writes=[Bo1])
                P.op("dve", lambda e, t_=t_, o2=o2: e.tensor_scalar(out=o2[:], in0=t_[:], scalar1=-1.0, scalar2=1.0, op0=ALU.mult, op1=ALU.add),
                     reads=[Bt], writes=[Bo2])
                P.dma("sp", hgT[2, blk], o1[:], reads=[Bo1])
                P.dma("sp", hgT[1, blk], o2[:], reads=[Bo2])


def mixer_consts():
    j = np.arange(128)[:, None]
    s = np.arange(128)[None, :]
    uneg = np.where(j >= s, -1.0, 0.0).astype(np.float32)
    c = np.arange(896)[None, :]
    maskw = ((c - 384) > j).astype(np.float32)
    hmask = ((j // 64 == s // 64) & (j <= s)).astype(np.float32)
    rmask = np.broadcast_to((np.arange(1024) % 64 != 0).astype(np.float32), (128, 1024))
    ident = np.eye(128, dtype=np.float32)
    return {"c_uneg": _bf(uneg), "c_maskw": _bf(maskw), "c_hmask": np.ascontiguousarray(hmask),
            "c_rmask": np.ascontiguousarray(rmask), "c_ident": _bf(ident)}


def build_B(Tn=T):
    nc = bass.Bass("TRN2", target_bir_lowering=False)
    qT = nc.dram_tensor("qT", [2, 128, Tn], BF16, kind="ExternalInput").ap()
    kT = nc.dram_tensor("kT", [2, 128, Tn], BF16, kind="ExternalInput").ap()
    vtok = nc.dram_tensor("vtok", [Tn, 256], BF16, kind="ExternalInput").ap()
    hq = nc.dram_tensor("hq", [2, 128, Tn], F32, kind="ExternalInput").ap()
    hk = nc.dram_tensor("hk", [2, 128, Tn], F32, kind="ExternalInput").ap()
    hg = nc.dram_tensor("hg", [2, 128, Tn], F32, kind="ExternalInput").ap()
    itok = nc.dram_tensor("itok", [Tn, 256], BF16, kind="ExternalInput").ap()
    gg = nc.dram_tensor("gg", [2, 128, Tn], F32, kind="ExternalInput").ap()
    nw = nc.dram_tensor("nw", [128, 4], F32, kind="ExternalInput").ap()
    cst = _const_aps(nc)
    mT = nc.dram_tensor("mT", [4, 128, Tn], BF16, kind="ExternalOutput").ap()
    C = Ctx(nc)
    emit_B(C, Tn, qT, kT, vtok, hq, hk, hg, itok, gg, nw, cst, mT[0:2], mT[2:4])
    C.finish()
    return nc


def _const_aps(nc):
    return (nc.dram_tensor("c_uneg", [128, 128], BF16, kind="ExternalInput").ap(),
            nc.dram_tensor("c_maskw", [128, 896], BF16, kind="ExternalInput").ap(),
            nc.dram_tensor("c_hmask", [128, 128], F32, kind="ExternalInput").ap(),
            nc.dram_tensor("c_rmask", [128, 1024], F32, kind="ExternalInput").ap(),
            nc.dram_tensor("c_ident", [128, 128], BF16, kind="ExternalInput").ap())


def emit_B(C, Tn, qT, kT, vtok, hq, hk, hg, itok, gg, nw, cst, m_sb, m_hg):
    c_uneg, c_maskw, c_hmask, c_rmask, c_ident = cst
    NQT = Tn // 512
    NKB = Tn // 128
    NSEG = Tn // 1024
    P = C.P
    ones, Bones = C.sb([128, 128], BF16, "ones")
    uneg, Buneg = C.sb([128, 128], BF16, "uneg")
    maskw, Bmaskw = C.sb([128, 896], BF16, "maskw")
    hmask, Bhmask = C.sb([128, 128], F32, "hmask")
    rmask, Brmask = C.sb([128, 1024], F32, "rmask")
    ident, Bident = C.sb([128, 128], BF16, "ident")
    nws, Bnws = C.sb([128, 4], F32, "nws")
    P.op("dve", lambda e: e.memset(ones[:], 1.0), writes=[Bones])
    P.dma("sp", uneg[:], c_uneg[:, :], writes=[Buneg])
    P.dma("sp", maskw[:], c_maskw[:, :], writes=[Bmaskw])
    P.dma("sp", hmask[:], c_hmask[:, :], writes=[Bhmask])
    P.dma("sp", rmask[:], c_rmask[:, :], writes=[Brmask])
    P.dma("sp", ident[:], c_ident[:, :], writes=[Bident])
    P.dma("sp", nws[:], nw[:, :], writes=[Bnws])
    pb = [C.ps([128, 512], F32, f"pb{i}") for i in range(8)]

    gS, BgS = C.sb([128, 1024], F32, "gS")
    qS, BqS = C.sb([128, 1024], F32, "qS")
    kS, BkS = C.sb([128, 1024], F32, "kS")
    ggS, BggS = C.sb([128, 1024], F32, "ggS")
    bS, BbS = C.sb([128, 1024], F32, "bS")
    ebS, BebS = C.sb([128, 1024], F32, "ebS")
    enbS, BenbS = C.sb([128, 1024], F32, "enbS")
    khS, BkhS = C.sb([128, 1024], F32, "khS")
    qhB, BqhB = C.sb([128, 1024], BF16, "qhB")
    khB, BkhB = C.sb([128, 1024], BF16, "khB")
    k2B, Bk2B = C.sb([128, 1024], BF16, "k2B")
    vH, BvH = C.sb([128, 8, 128], BF16, "vH")
    oS, BoS = C.sb([128, 1024], F32, "oS")
    sqS, BsqS = C.sb([128, 1024], BF16, "sqS")
    rS, BrS = C.sb([128, 1024], F32, "rS")
    yS, ByS = C.sb([128, 1024], F32, "yS")
    yB, ByB = C.sb([128, 1024], BF16, "yB")
    S32, BS32 = C.sb([128, 128], F32, "S32")
    Sb, BSb = C.sb([128, 128], BF16, "Sb")
    scm = [C.sb([128, 128], BF16, f"scm{i}") for i in range(2)]
    k2T = [C.sb([128, 128], BF16, f"k2T{i}") for i in range(2)]
    (p_sc, Bp_sc), (p_tr, Bp_tr), (p_o, Bp_o), (p_s1, Bp_s1), (p_s2, Bp_s2), (p_n0, Bp_n0), (p_n1, Bp_n1) = pb[0:7]
    p_trb = p_tr[:, 0:64].bitcast(BF16)
    for hh in range(2):
        P.op("dve", lambda e: e.memset(S32[:], 0.0), writes=[BS32])
        P.op("dve", lambda e: e.memset(Sb[:], 0.0), writes=[BSb])
        for sg in range(NSEG):
            t0 = sg * 1024
            P.dma("sp", gS[:], hg[hh, :, t0:t0 + 1024], writes=[BgS])
            P.dma("act", qS[:], hq[hh, :, t0:t0 + 1024], writes=[BqS])
            P.dma("sp", kS[:], hk[hh, :, t0:t0 + 1024], writes=[BkS])
            P.dma("act", ggS[:], gg[hh, :, t0:t0 + 1024], writes=[BggS])
            P.dma("sp", vH[:], itok[t0:t0 + 1024, hh * 128:(hh + 1) * 128].rearrange("(b p) c -> p b c", p=128), writes=[BvH])
            P.op("dve", lambda e: e.tensor_tensor_scan(out=bS[:], data0=rmask[:], data1=gS[:], initial=0.0, op0=ALU.mult, op1=ALU.add),
                 reads=[Brmask, BgS], writes=[BbS])
            P.op("act", lambda e: e.activation(out=ebS[:], in_=bS[:], func=AF.Exp), reads=[BbS], writes=[BebS])
            P.op("act", lambda e: e.activation(out=enbS[:], in_=bS[:], func=AF.Exp, scale=-1.0), reads=[BbS], writes=[BenbS])
            P.op("dve", lambda e: e.tensor_tensor(out=qhB[:], in0=qS[:], in1=ebS[:], op=ALU.mult), reads=[BqS, BebS], writes=[BqhB])
            P.op(C.pool_eng, lambda e: e.tensor_tensor(out=khS[:], in0=kS[:], in1=enbS[:], op=ALU.mult), reads=[BkS, BenbS], writes=[BkhS])
            P.op("act", lambda e: e.copy(out=khB[:], in_=khS[:]), reads=[BkhS], writes=[BkhB])
            for ck in range(16):
                P.op("dve", lambda e, ck=ck: e.tensor_scalar(out=k2B[:, ck * 64:(ck + 1) * 64], in0=khS[:, ck * 64:(ck + 1) * 64],
                                                              scalar1=ebS[:, ck * 64 + 63:ck * 64 + 64], scalar2=None, op0=ALU.mult),
                     reads=[BkhS, BebS], writes=[Bk2B], partial=True)
            for bi in range(8):
                c0 = bi * 128
                sm, Bsm = scm[bi % 2]
                kt, Bkt = k2T[bi % 2]
                P.op("pe", lambda e, c0=c0: e.matmul(p_sc[:, 0:128], lhsT=khB[:, c0:c0 + 128], rhs=qhB[:, c0:c0 + 128], start=True, stop=True),
                     reads=[BkhB, BqhB], writes=[Bp_sc])
                P.op("dve", lambda e, sm=sm: e.tensor_tensor(out=sm[:], in0=p_sc[:, 0:128], in1=hmask[:], op=ALU.mult),
                     reads=[Bp_sc, Bhmask], writes=[Bsm])
                P.op("pe", lambda e, c0=c0: e.transpose(p_trb, k2B[:, c0:c0 + 128], ident[:]), reads=[Bk2B, Bident], writes=[Bp_tr])
                P.op("act", lambda e, kt=kt: e.copy(out=kt[:], in_=p_trb), reads=[Bp_tr], writes=[Bkt])
                P.op("pe", lambda e, sm=sm, bi=bi: e.matmul(p_o[:, 0:128], lhsT=vH[:, bi, :], rhs=sm[:], start=True, stop=False),
                     reads=[BvH, Bsm], writes=[Bp_o])
                for half in range(2):
                    ps_, Bps_ = (p_s1, Bp_s1) if half == 0 else (p_s2, Bp_s2)
                    lo = half * 64
                    P.op("pe", lambda e, c0=c0, lo=lo, half=half: e.matmul(p_o[:, lo:lo + 64], lhsT=Sb[:], rhs=qhB[:, c0 + lo:c0 + lo + 64],
                                                                            start=False, stop=(half == 1)),
                         reads=[BSb, BqhB], writes=[Bp_o], partial=True)
                    P.op("pe", lambda e, kt=kt, bi=bi, lo=lo, ps_=ps_: e.matmul(ps_[:, 0:128], lhsT=kt[lo:lo + 64, :], rhs=vH[lo:lo + 64, bi, :],
                                                                                  start=True, stop=True),
                         reads=[Bkt, BvH], writes=[Bps_])
                    P.op("dve", lambda e, ps_=ps_, c0=c0, lo=lo: e.scalar_tensor_tensor(out=S32[:], in0=S32[:], scalar=ebS[:, c0 + lo + 63:c0 + lo + 64],
                                                                                         in1=ps_[:, 0:128], op0=ALU.mult, op1=ALU.add),
                         reads=[BS32, BebS, Bps_], writes=[BS32])
                    P.op("act", lambda e: e.copy(out=Sb[:], in_=S32[:]), reads=[BS32], writes=[BSb])
                P.op("act", lambda e, c0=c0: e.copy(out=oS[:, c0:c0 + 128], in_=p_o[:, 0:128]), reads=[Bp_o], writes=[BoS], partial=True)
            P.op("act", lambda e: e.activation(out=sqS[:], in_=oS[:], func=AF.Square), reads=[BoS], writes=[BsqS])
            for hf, (pn, Bpn) in enumerate(((p_n0, Bp_n0), (p_n1, Bp_n1))):
                P.op("pe", lambda e, pn=pn, hf=hf: e.matmul(pn[:], lhsT=ones[:], rhs=sqS[:, hf * 512:(hf + 1) * 512], start=True, stop=True),
                     reads=[Bones, BsqS], writes=[Bpn])
                P.op("act", lambda e, pn=pn, hf=hf: e.activation(out=rS[:, hf * 512:(hf + 1) * 512], in_=pn[:], func=AF.Ln, scale=1.0 / 128, bias=EPS),
                     reads=[Bpn], writes=[BrS], partial=True)
            P.op("act", lambda e: e.activation(out=rS[:], in_=rS[:], func=AF.Exp, scale=-0.5), reads=[BrS], writes=[BrS])
            P.op("dve", lambda e, hh=hh: e.scalar_tensor_tensor(out=yS[:], in0=oS[:], scalar=nws[:, 2 + hh:3 + hh], in1=rS[:], op0=ALU.mult, op1=ALU.mult),
                 reads=[BoS, Bnws, BrS], writes=[ByS])
            P.op("dve", lambda e: e.tensor_tensor(out=yB[:], in0=yS[:], in1=ggS[:], op=ALU.mult), reads=[ByS, BggS], writes=[ByB])
            P.dma("sp", m_hg[hh, :, t0:t0 + 1024], yB[:], reads=[ByB])

    qA, BqA = C.sb([128, 2, Tn], BF16, "qA")
    kA, BkA = C.sb([128, 2, Tn], BF16, "kA")
    vA, BvA = C.sb([128, NKB, 256], BF16, "vA")
    for h in range(2):
        P.dma("sp", qA[:, h, :], qT[h], writes=[BqA], partial=True)
        P.dma("act", kA[:, h, :], kT[h], writes=[BkA], partial=True)
    for q4 in range(4):
        n4 = NKB // 4
        P.dma("sp", vA[:, q4 * n4:(q4 + 1) * n4, :], vtok[q4 * n4 * 128:(q4 + 1) * n4 * 128, :].rearrange("(b p) c -> p b c", p=128),
              writes=[BvA], partial=True)
    eS = [C.sb([128, 512], F32, f"eS{i}") for i in range(2)]
    spB = [C.sb([128, 512], BF16, f"spB{i}") for i in range(3)]
    spM = [C.sb([128, 512], BF16, f"spM{i}") for i in range(2)]
    lwS = [C.sb([128, 512], F32, f"lwS{i}") for i in range(2)]
    aB = [C.sb([128, 512], BF16, f"aB{i}") for i in range(3)]
    aM = [C.sb([128, 512], BF16, f"aM{i}") for i in range(2)]
    racc = [C.sb([128, 512], F32, f"racc{i}") for i in range(2)]
    sqA = [C.sb([128, 512], BF16, f"sqA{i}") for i in range(2)]
    rA = [C.sb([128, 512], F32, f"rA{i}") for i in range(2)]
    yA = [C.sb([128, 512], BF16, f"yA{i}") for i in range(2)]
    Pb = pb[0:3]
    Tb = pb[3:5]
    Ob = pb[5:7]
    Nb = pb[7]
    blocks = []
    for h in range(2):
        for qi in range(NQT):
            nkb = 4 * qi + 4
            for kb in reversed(range(nkb)):
                blocks.append((h, qi, kb, kb == nkb - 1, kb == 0))
    st = {}

    def s1(i):
        h, qi, kb, first, last = blocks[i]
        p_, Bp = Pb[i % 3]
        e_, Be = eS[i % 2]
        sp_, Bsp = spB[i % 3]
        qt = (h * NQT + qi)
        if first:
            r_, Br = racc[qt % 2]
            P.op(C.pool_eng, lambda e, r_=r_: e.memset(r_[:], 0.0), writes=[Br])
        P.op("pe", lambda e, p_=p_, h=h, kb=kb, qi=qi: e.matmul(p_[:], lhsT=kA[:, h, kb * 128:(kb + 1) * 128], rhs=qA[:, h, qi * 512:(qi + 1) * 512],
                                                              start=True, stop=False),
             reads=[BkA, BqA], writes=[Bp])
        P.op("act", lambda e, p_=p_, e_=e_: e.activation(out=e_[:], in_=p_[:], func=AF.Exp), reads=[Bp], writes=[Be])
        P.op("act", lambda e, e_=e_, sp_=sp_: e.activation(out=sp_[:], in_=e_[:], func=AF.Ln, bias=1.0), reads=[Be], writes=[Bsp])
        di = kb - 4 * qi
        if di >= 0:
            m_, Bm = spM[i % 2]
            off = 384 - 128 * di
            P.op(C.pool_eng, lambda e, sp_=sp_, m_=m_, off=off: e.tensor_tensor(out=m_[:], in0=sp_[:], in1=maskw[:, off:off + 512], op=ALU.mult),
                 reads=[Bsp, Bmaskw], writes=[Bm])
            st[i] = (m_, Bm)
        else:
            st[i] = (sp_, Bsp)

    def s2(i):
        h, qi, kb, first, last = blocks[i]
        p_, Bp = Pb[i % 3]
        t_, Bt = Tb[i % 2]
        sp_, Bsp = st[i]
        lw_, Blw = lwS[i % 2]
        a_, Ba = aB[i % 3]
        qt = (h * NQT + qi)
        r_, Br = racc[qt % 2]
        P.op("pe", lambda e, p_=p_, sp_=sp_: e.matmul(p_[:], lhsT=uneg[:], rhs=sp_[:], start=False, stop=True),
             reads=[Buneg, Bsp], writes=[Bp], partial=True)
        if not last:
            P.op("pe", lambda e, t_=t_, sp_=sp_: e.matmul(t_[:], lhsT=ones[:], rhs=sp_[:], start=True, stop=True),
                 reads=[Bones, Bsp], writes=[Bt])
        P.op("dve", lambda e, p_=p_, r_=r_, lw_=lw_: e.tensor_tensor(out=lw_[:], in0=p_[:], in1=r_[:], op=ALU.subtract),
             reads=[Bp, Br], writes=[Blw])
        if not last:
            P.op("dve", lambda e, t_=t_, r_=r_: e.tensor_tensor(out=r_[:], in0=t_[:], in1=r_[:], op=ALU.add), reads=[Bt, Br], writes=[Br])
        P.op("act", lambda e, lw_=lw_, a_=a_: e.activation(out=a_[:], in_=lw_[:], func=AF.Exp), reads=[Blw], writes=[Ba])
        di = kb - 4 * qi
        if di >= 0:
            m_, Bm = aM[i % 2]
            off = 384 - 128 * di
            P.op(C.pool_eng, lambda e, a_=a_, m_=m_, off=off: e.tensor_tensor(out=m_[:], in0=a_[:], in1=maskw[:, off:off + 512], op=ALU.mult),
                 reads=[Ba, Bmaskw], writes=[Bm])
            st[i] = (m_, Bm)
        else:
            st[i] = (a_, Ba)

    def s3(i):
        h, qi, kb, first, last = blocks[i]
        a_, Ba = st[i]
        qt = (h * NQT + qi)
        o_, Bo = Ob[qt % 2]
        P.op("pe", lambda e, o_=o_, a_=a_, h=h, kb=kb, first=first, last=last: e.matmul(o_[:], lhsT=vA[:, kb, h * 128:(h + 1) * 128], rhs=a_[:],
                                                                                    start=first, stop=last),
             reads=[BvA, Ba], writes=[Bo], partial=not first)
        if last:
            s_, Bs = sqA[qt % 2]
            rr, Brr = rA[qt % 2]
            y_, By = yA[qt % 2]
            n_, Bn = Nb
            P.op("act", lambda e, o_=o_, s_=s_: e.activation(out=s_[:], in_=o_[:], func=AF.Square), reads=[Bo], writes=[Bs])
            P.op("pe", lambda e, s_=s_: e.matmul(n_[:], lhsT=ones[:], rhs=s_[:], start=True, stop=True), reads=[Bones, Bs], writes=[Bn])
            P.op("act", lambda e, rr=rr: e.activation(out=rr[:], in_=n_[:], func=AF.Ln, scale=1.0 / 128, bias=EPS), reads=[Bn], writes=[Brr])
            P.op("act", lambda e, rr=rr: e.activation(out=rr[:], in_=rr[:], func=AF.Exp, scale=-0.5), reads=[Brr], writes=[Brr])
            P.op("dve", lambda e, o_=o_, rr=rr, y_=y_, h=h: e.scalar_tensor_tensor(out=y_[:], in0=o_[:], scalar=nws[:, h:h + 1], in1=rr[:],
                                                                                 op0=ALU.mult, op1=ALU.mult),
                 reads=[Bo, Bnws, Brr], writes=[By])
            P.dma("sp", m_sb[h, :, qi * 512:(qi + 1) * 512], y_[:], reads=[By])

    nb_ = len(blocks)
    for i in range(nb_ + 2):
        if i < nb_:
            s1(i)
        if 0 <= i - 1 < nb_:
            s2(i - 1)
        if 0 <= i - 2 < nb_:
            s3(i - 2)


NTH = NT + 2
NJ = DFF // 128
DBGK = "Internal"


def build_C():
    nc = bass.Bass("TRN2", target_bir_lowering=False)
    mTh = nc.dram_tensor("mTh", [D, NTH], BF16, kind="ExternalInput").ap()
    hTh = nc.dram_tensor("hTh", [D, NTH], F32, kind="ExternalInput").ap()
    w_out = nc.dram_tensor("w_out", [D, D], F32, kind="ExternalInput").ap()
    n2w = nc.dram_tensor("n2w", [128, KC], F32, kind="ExternalInput").ap()
    w_up = nc.dram_tensor("w_up", [D, 2 * DFF], F32, kind="ExternalInput").ap()
    cw = nc.dram_tensor("cw", [128, 2 * NJ, 3], F32, kind="ExternalInput").ap()
    cbv = nc.dram_tensor("cb", [128, 2 * NJ], F32, kind="ExternalInput").ap()
    w_down = nc.dram_tensor("w_down", [DFF, D], F32, kind="ExternalInput").ap()
    h2T = nc.dram_tensor("h2T", [D, NT], F32, kind="ExternalOutput").ap()
    hmid = nc.dram_tensor("hmid", [D, NTH], F32, kind=DBGK).ap()
    aT = nc.dram_tensor("aT", [NJ, 128, NT], BF16, kind=DBGK).ap()
    C = Ctx(nc)
    emit_C(C, mTh, hTh, w_out, n2w, w_up, cw, cbv, w_down, h2T, hmid, aT)
    C.finish()
    return nc


def emit_C(C, mTh, hTh, w_out, n2w, w_up, cw, cbv, w_down, h2T, hmid, aT):
    P = C.P
    ones, Bones = C.sb([128, 128], BF16, "ones")
    n2, Bn2 = C.sb([128, KC], F32, "n2")
    cws, Bcws = C.sb([128, 2 * NJ, 3], F32, "cws")
    cbs, Bcbs = C.sb([128, 2 * NJ], F32, "cbs")
    arena, BX = C.sb([128, NJ * 512], BF16, "arena")
    X = arena[:, 0:KC * NTH].rearrange("p (k t) -> p k t", t=NTH)
    AH = arena[:, 0:NJ * 512].rearrange("p (k t) -> p k t", t=512)
    NW = 2 if C.arena else 3
    wraw = [C.sb([128, 43 * 256], BF16, f"wr{i}") for i in range(NW)]
    mA = [C.ps([128, NT], F32, f"mA{i}") for i in range(2)]
    hb, _ = C.ps([128, 512], F32, "hb")
    Bhb = [Buf(f"hb{i}") for i in range(4)]
    hq, Bssqh = C.ps([128, 512], F32, "hq")
    ssq, Bssq = C.ps([128, NT], F32, "ssq")
    rstd, Brstd = C.sb([128, NTH], F32, "rstd")
    stg = [C.sb([128, NTH], F32, f"stg{i}") for i in range(2)]
    sqb = [C.sb([128, NTH], BF16, f"sqb{i}") for i in range(2)]
    uS = [C.sb([128, NTH], F32, f"uS{i}") for i in range(2)]
    uc, Buc = C.sb([128, NT], F32, "uc")
    tm, Btm = C.sb([128, NT], F32, "tm")
    sg, Bsg = C.sb([128, NT], F32, "sg")
    aB = [C.sb([128, NT], BF16, f"aB{i}") for i in range(2)]
    Bhm = [Buf(f"hm{i}") for i in range(KC)]
    BaT = [Buf(f"aT{i}") for i in range(NJ)]

    P.op("dve", lambda e: e.memset(ones[:], 1.0), writes=[Bones])
    P.dma("sp", n2[:], n2w[:, :], writes=[Bn2])
    P.dma("sp", cws[:], cw[:, :, :], writes=[Bcws])
    P.dma("sp", cbs[:], cbv[:, :], writes=[Bcbs])
    for kc in range(KC):
        P.dma("sp" if kc % 2 == 0 else "act", X[:, kc, :], mTh[kc * 128:(kc + 1) * 128, :], writes=[BX], partial=True)

    segs = ((0, 2, None), (2, 514, 0), (514, 1026, 1))

    def mm_block(a, Ba, hslot, lhs_fn, nk, rhs):
        for kc in range(nk):
            for (lo, hi, hf) in segs:
                if hf is None:
                    P.op("pe", lambda e, kc=kc, lo=lo, hi=hi: e.matmul(hb[:, 2 * hslot:2 * hslot + 2], lhsT=lhs_fn(kc), rhs=rhs[:, kc, lo:hi],
                                                                     start=(kc == 0), stop=(kc == nk - 1)),
                         reads=[BX, Bw_cur[0]], writes=[Bhb[hslot]], partial=True)
                else:
                    P.op("pe", lambda e, kc=kc, lo=lo, hi=hi, hf=hf: e.matmul(a[:, hf * 512:(hf + 1) * 512], lhsT=lhs_fn(kc), rhs=rhs[:, kc, lo:hi],
                                                                            start=(kc == 0), stop=(kc == nk - 1)),
                         reads=[BX, Bw_cur[0]], writes=[Ba], partial=True)

    Bw_cur = [None]
    wv = w_out.rearrange("(kc p) c -> p kc c", p=128)
    nblk = 0
    for g in range(D // 256):
        wr, Bw = wraw[g % NW]
        w_ = wr[:, 0:KC * 256].rearrange("p (k c) -> p k c", c=256)
        for q4 in range(4):
            C.wload(w_[:, q4 * 8:(q4 + 1) * 8, :], wv[:, q4 * 8:(q4 + 1) * 8, g * 256:(g + 1) * 256], 8, 256, Bw, "act" if g % 2 else "dve")
        Bw_cur[0] = Bw
        for cb in range(2):
            n = 2 * g + cb
            a, Ba = mA[nblk % 2]
            hs = nblk % 4
            nblk += 1
            r_, Br = stg[n % 2]
            s_, Bs = sqb[n % 2]
            P.dma("sp", r_[:], hTh[n * 128:(n + 1) * 128, :], writes=[Br])
            mm_block(a, Ba, hs, lambda kc, w_=w_, cb=cb: w_[:, kc, cb * 128:(cb + 1) * 128], KC, X)
            P.op("dve", lambda e, a=a, r_=r_: e.tensor_tensor(out=r_[:, 2:NTH], in0=a[:], in1=r_[:, 2:NTH], op=ALU.add), reads=[Ba, Br], writes=[Br])
            P.op("dve", lambda e, r_=r_, hs=hs: e.tensor_tensor(out=r_[:, 0:2], in0=hb[:, 2 * hs:2 * hs + 2], in1=r_[:, 0:2], op=ALU.add),
                 reads=[Bhb[hs], Br], writes=[Br], partial=True)
            P.dma("sp", hmid[n * 128:(n + 1) * 128, :], r_[:], reads=[Br], writes=[Bhm[n]])
            P.op("act", lambda e, r_=r_, s_=s_: e.activation(out=s_[:], in_=r_[:], func=AF.Square), reads=[Br], writes=[Bs])
            for (lo, hi, hf) in segs:
                if hf is None:
                    P.op("pe", lambda e, s_=s_, n=n: e.matmul(hq[:, 0:2], lhsT=ones[:], rhs=s_[:, 0:2], start=(n == 0), stop=(n == KC - 1)),
                         reads=[Bones, Bs], writes=[Bssqh], partial=True)
                else:
                    P.op("pe", lambda e, s_=s_, n=n, lo=lo, hi=hi, hf=hf: e.matmul(ssq[:, hf * 512:(hf + 1) * 512], lhsT=ones[:], rhs=s_[:, lo:hi],
                                                                                 start=(n == 0), stop=(n == KC - 1)),
                         reads=[Bones, Bs], writes=[Bssq], partial=True)
    P.op("act", lambda e: e.activation(out=rstd[:, 2:NTH], in_=ssq[:], func=AF.Ln, scale=1.0 / D, bias=EPS), reads=[Bssq], writes=[Brstd])
    P.op("act", lambda e: e.activation(out=rstd[:, 0:2], in_=hq[:, 0:2], func=AF.Ln, scale=1.0 / D, bias=EPS), reads=[Bssqh], writes=[Brstd], partial=True)
    P.op("act", lambda e: e.activation(out=rstd[:], in_=rstd[:], func=AF.Exp, scale=-0.5), reads=[Brstd], writes=[Brstd])
    for kc in range(KC):
        r_, Br = stg[kc % 2]
        P.dma("sp" if kc % 2 == 0 else "act", r_[:], hmid[kc * 128:(kc + 1) * 128, :], reads=[Bhm[kc]], writes=[Br])
        P.op("dve", lambda e, r_=r_, kc=kc: e.scalar_tensor_tensor(out=X[:, kc, :], in0=r_[:], scalar=n2[:, kc:kc + 1], in1=rstd[:], op0=ALU.mult, op1=ALU.mult),
             reads=[Br, Bn2, Brstd], writes=[BX], partial=(kc > 0))
    uv = w_up.rearrange("(kc p) c -> p kc c", p=128)
    for j in range(NJ):
        wr, Bw = wraw[j % NW]
        w_ = wr[:, 0:KC * 256].rearrange("p (k c) -> p k c", c=256)
        for ub in range(2):
            c0 = ub * DFF + j * 128
            for q4 in range(4):
                C.wload(w_[:, q4 * 8:(q4 + 1) * 8, ub * 128:(ub + 1) * 128], uv[:, q4 * 8:(q4 + 1) * 8, c0:c0 + 128], 8, 128, Bw, "act" if j % 2 else "dve")
        Bw_cur[0] = Bw
        for ub in range(2):
            ch = ub * NJ + j
            a, Ba = mA[nblk % 2]
            hs = nblk % 4
            nblk += 1
            u_, Bu = uS[ub]
            mm_block(a, Ba, hs, lambda kc, w_=w_, ub=ub: w_[:, kc, ub * 128:(ub + 1) * 128], KC, X)
            P.op("act", lambda e, a=a, u_=u_: e.copy(out=u_[:, 2:NTH], in_=a[:]), reads=[Ba], writes=[Bu])
            P.op("dve", lambda e, u_=u_, hs=hs: e.tensor_copy(out=u_[:, 0:2], in_=hb[:, 2 * hs:2 * hs + 2]), reads=[Bhb[hs]], writes=[Bu], partial=True)
            P.op("dve", lambda e, u_=u_, ch=ch: e.tensor_scalar(out=uc[:], in0=u_[:, 2:NTH], scalar1=cws[:, ch, 2:3], scalar2=cbs[:, ch:ch + 1],
                                                                 op0=ALU.mult, op1=ALU.add), reads=[Bu, Bcws, Bcbs], writes=[Buc])
            P.op("dve", lambda e, u_=u_, ch=ch: e.scalar_tensor_tensor(out=uc[:], in0=u_[:, 1:NTH - 1], scalar=cws[:, ch, 1:2], in1=uc[:],
                                                                        op0=ALU.mult, op1=ALU.add), reads=[Bu, Bcws, Buc], writes=[Buc])
            P.op("dve", lambda e, u_=u_, ch=ch: e.scalar_tensor_tensor(out=uc[:], in0=u_[:, 0:NT], scalar=cws[:, ch, 0:1], in1=uc[:],
                                                                        op0=ALU.mult, op1=ALU.add), reads=[Bu, Bcws, Buc], writes=[Buc])
            if ub == 0:
                P.op("act", lambda e: e.activation(out=tm[:], in_=uc[:], func=AF.Exp, scale=-1.0), reads=[Buc], writes=[Btm])
                P.op("dve", lambda e: e.tensor_scalar_add(out=tm[:], in0=tm[:], scalar1=1.0), reads=[Btm], writes=[Btm])
                P.op("dve", lambda e: e.reciprocal(out=tm[:], in_=tm[:]), reads=[Btm], writes=[Btm])
                P.op("dve", lambda e: e.tensor_tensor(out=sg[:], in0=uc[:], in1=tm[:], op=ALU.mult), reads=[Buc, Btm], writes=[Bsg])
            else:
                ab, Bab = aB[j % 2]
                P.op("dve", lambda e, ab=ab: e.tensor_tensor(out=ab[:], in0=uc[:], in1=sg[:], op=ALU.mult), reads=[Buc, Bsg], writes=[Bab])
                P.dma("sp", aT[j], ab[:], reads=[Bab], writes=[BaT[j]])
    dv = w_down.rearrange("(kc p) c -> p kc c", p=128)
    for half in range(2):
        for j0 in range(0, NJ, 8):
            j1 = min(NJ, j0 + 8)
            P.dma("sp", AH[:, j0:j1, :], aT[j0:j1, :, half * 512:(half + 1) * 512].rearrange("j p t -> p j t"),
                  reads=BaT[j0:j1], writes=[BX], partial=(j0 > 0))
        for g in range(D // 256):
            a, Ba = mA[g % 2]
            for kh in range(2):
                wr, Bw = wraw[(2 * g + kh) % NW]
                w_ = wr[:, 0:43 * 256].rearrange("p (k c) -> p k c", c=256)
                for (k0, k1) in ((0, 8), (8, 16), (16, 24), (24, 32), (32, 40), (40, 43)):
                    C.wload(w_[:, k0:k1, :], dv[:, kh * 43 + k0:kh * 43 + k1, g * 256:(g + 1) * 256], k1 - k0, 256, Bw, "act" if (2 * g + kh) % 2 else "dve")
                for cb in range(2):
                    for kc in range(43):
                        kk = kh * 43 + kc
                        P.op("pe", lambda e, a=a, w_=w_, kc=kc, kk=kk, cb=cb: e.matmul(a[:, cb * 512:(cb + 1) * 512], lhsT=w_[:, kc, cb * 128:(cb + 1) * 128],
                                                                                 rhs=AH[:, kk, :], start=(kk == 0), stop=(kk == NJ - 1)),
                             reads=[BX, Bw], writes=[Ba], partial=True)
            for cb in range(2):
                n = 2 * g + cb
                r_, Br = stg[n % 2]
                P.dma("act", r_[:, 0:512], hmid[n * 128:(n + 1) * 128, 2 + half * 512:2 + (half + 1) * 512], reads=[Bhm[n]], writes=[Br])
                P.op("dve", lambda e, a=a, r_=r_, cb=cb: e.tensor_tensor(out=r_[:, 0:512], in0=a[:, cb * 512:(cb + 1) * 512], in1=r_[:, 0:512], op=ALU.add),
                     reads=[Ba, Br], writes=[Br])
                P.dma("sp", h2T[n * 128:(n + 1) * 128, half * 512:(half + 1) * 512], r_[:, 0:512], reads=[Br])


def build_D():
    nc = bass.Bass("TRN2", target_bir_lowering=False)
    hT = nc.dram_tensor("hT", [D, NT], F32, kind="ExternalInput").ap()
    fw = nc.dram_tensor("fw", [128, KC], F32, kind="ExternalInput").ap()
    oT = nc.dram_tensor("oT", [D, NT], F32, kind="ExternalOutput").ap()
    C = Ctx(nc)
    emit_D(C, hT, fw, oT)
    C.finish()
    return nc


def emit_D(C, hT, fw, oT):
    P = C.P
    ones, Bones = C.sb([128, 128], BF16, "ones")
    n1, Bn1 = C.sb([128, KC], F32, "n1")
    rstd, Brstd = C.sb([128, NT], F32, "rstd")
    t0, Bt0 = C.sb([128, NT], F32, "t0")
    ch = [C.sb([128, NT], F32, f"ch{i}") for i in range(3)]
    sq = [C.sb([128, NT], BF16, f"sq{i}") for i in range(2)]
    ob = [C.sb([128, NT], F32, f"ob{i}") for i in range(2)]
    ssq, Bssq = C.ps([128, NT], F32, "ssq")
    P.op("dve", lambda e: e.memset(ones[:], 1.0), writes=[Bones])
    P.dma("sp", n1[:], fw[:, :], writes=[Bn1])
    for kc in range(KC):
        c, Bc = ch[kc % 3]
        s, Bs = sq[kc % 2]
        P.dma("sp" if kc % 2 == 0 else "act", c[:], hT[kc * 128:(kc + 1) * 128, :], writes=[Bc])
        P.op("act", lambda e, c=c, s=s: e.activation(out=s[:], in_=c[:], func=AF.Square), reads=[Bc], writes=[Bs])
        for hf in range(2):
            P.op("pe", lambda e, s=s, hf=hf, kc=kc: e.matmul(ssq[:, hf * 512:(hf + 1) * 512], lhsT=ones[:], rhs=s[:, hf * 512:(hf + 1) * 512],
                                                          start=(kc == 0), stop=(kc == KC - 1)),
                 reads=[Bs, Bones], writes=[Bssq], partial=True)
    P.op("act", lambda e: e.activation(out=t0[:], in_=ssq[:], func=AF.Ln, scale=1.0 / D, bias=EPS), reads=[Bssq], writes=[Bt0])
    P.op("act", lambda e: e.activation(out=rstd[:], in_=t0[:], func=AF.Exp, scale=-0.5), reads=[Bt0], writes=[Brstd])
    for kc in range(KC):
        c, Bc = ch[kc % 3]
        o_, Bo = ob[kc % 2]
        P.dma("sp" if kc % 2 == 0 else "act", c[:], hT[kc * 128:(kc + 1) * 128, :], writes=[Bc])
        P.op("dve", lambda e, c=c, o_=o_, kc=kc: e.scalar_tensor_tensor(out=o_[:], in0=c[:], scalar=n1[:, kc:kc + 1], in1=rstd[:], op0=ALU.mult, op1=ALU.mult),
             reads=[Bc, Bn1, Brstd], writes=[Bo])
        P.dma("sp", oT[kc * 128:(kc + 1) * 128, :], o_[:], reads=[Bo])


TP = T + 2
NTILE = T // NT


def build_L():
    nc = bass.Bass("TRN2", target_bir_lowering=False)
    ds = bass.ds
    hpad = nc.dram_tensor("hpad", [D, TP], F32, kind="ExternalInput").ap()
    w_in = nc.dram_tensor("w_in", [D, INC], F32, kind="ExternalInput").ap()
    n1w = nc.dram_tensor("n1w", [128, KC], F32, kind="ExternalInput").ap()
    lbp = nc.dram_tensor("lbp", [128, 16, DEPTH], F32, kind="ExternalInput").ap()
    lsel = nc.dram_tensor("lsel", [128, 16, DEPTH], F32, kind="ExternalInput").ap()
    nwa = nc.dram_tensor("nwa", [NCORE, 128, 4], F32, kind="ExternalInput").ap()
    cst = _const_aps(nc)
    zpad = nc.dram_tensor("zpad", [D, 2], BF16, kind="ExternalInput").ap()
    w_out = nc.dram_tensor("w_out", [D, D], F32, kind="ExternalInput").ap()
    n2w = nc.dram_tensor("n2w", [128, KC], F32, kind="ExternalInput").ap()
    w_up = nc.dram_tensor("w_up", [D, 2 * DFF], F32, kind="ExternalInput").ap()
    cw = nc.dram_tensor("cw", [128, 2 * NJ, 3], F32, kind="ExternalInput").ap()
    cbv = nc.dram_tensor("cb", [128, 2 * NJ], F32, kind="ExternalInput").ap()
    w_down = nc.dram_tensor("w_down", [DFF, D], F32, kind="ExternalInput").ap()
    h2T = nc.dram_tensor("h2T", [D, T], F32, kind="ExternalOutput").ap()
    qkT = nc.dram_tensor("s_qkT", [32 * 128, T], BF16).ap()
    vtok = nc.dram_tensor("s_vtok", [T, SBW], BF16).ap()
    hgT = nc.dram_tensor("s_hgT", [3 * 16 * 128, T], F32).ap()
    itok = nc.dram_tensor("s_itok", [T, HGW], BF16).ap()
    ggT = nc.dram_tensor("s_ggT", [16 * 128, T], F32).ap()
    mTp = nc.dram_tensor("s_mTp", [D, TP], BF16).ap()
    a_h = nc.dram_tensor("a_h", [D, NT], F32).ap()
    a_qk = nc.dram_tensor("a_qk", [32, 128, NT], BF16).ap()
    a_v = nc.dram_tensor("a_v", [NT, SBW], BF16).ap()
    a_hg = nc.dram_tensor("a_hg", [3, 16, 128, NT], F32).ap()
    a_i = nc.dram_tensor("a_i", [NT, HGW], BF16).ap()
    a_gg = nc.dram_tensor("a_gg", [16, 128, NT], F32).ap()
    b_q = nc.dram_tensor("b_q", [2, 128, T], BF16).ap()
    b_k = nc.dram_tensor("b_k", [2, 128, T], BF16).ap()
    b_v = nc.dram_tensor("b_v", [T, 256], BF16).ap()
    b_hq = nc.dram_tensor("b_hq", [2, 128, T], F32).ap()
    b_hk = nc.dram_tensor("b_hk", [2, 128, T], F32).ap()
    b_hg = nc.dram_tensor("b_hg", [2, 128, T], F32).ap()
    b_i = nc.dram_tensor("b_i", [T, 256], BF16).ap()
    b_gg = nc.dram_tensor("b_gg", [2, 128, T], F32).ap()
    b_nw = nc.dram_tensor("b_nw", [128, 4], F32).ap()
    b_m = nc.dram_tensor("b_m", [4, 128, T], BF16).ap()
    c_m = nc.dram_tensor("c_m", [D, NTH], BF16).ap()
    c_h = nc.dram_tensor("c_h", [D, NTH], F32).ap()
    c_o = nc.dram_tensor("c_o", [D, NT], F32).ap()
    hmid = nc.dram_tensor("s_hmid", [D, NTH], F32).ap()
    aT = nc.dram_tensor("s_aT", [NJ, 128, NT], BF16).ap()

    C = Ctx(nc, arena=True)
    C.begin()
    C.P.dma("sp", mTp[:, 0:2], zpad[:, :])
    C.end_body()
    fl3 = lambda ap: ap.rearrange("h p t -> (h p) t")
    with nc.Fori(0, NTILE) as i:
        C.begin()
        P = C.P
        P.dma("sp", a_h[:, :], hpad[:, ds(i * NT + 2, NT)])
        P.barrier()
        emit_A(C, a_h, w_in, n1w, lbp, lsel, a_qk, a_v, a_hg, a_i, a_gg)
        P.barrier()
        tsl = ds(i * NT, NT)
        P.dma("sp", qkT[:, tsl], fl3(a_qk))
        P.dma("act", vtok[tsl, :], a_v[:, :])
        P.dma("sp", hgT[:, tsl], a_hg.rearrange("a h p t -> (a h p) t"))
        P.dma("act", itok[tsl, :], a_i[:, :])
        P.dma("act", ggT[:, tsl], fl3(a_gg))
        C.end_body()
    with nc.Fori(0, NCORE) as i:
        C.begin()
        P = C.P
        g256 = ds(i * 256, 256)
        P.dma("sp", fl3(b_q), qkT[g256, :])
        P.dma("act", fl3(b_k), qkT[ds(i * 256 + 2048, 256), :])
        P.dma("sp", b_v[0:T // 2, :], vtok[0:T // 2, g256])
        P.dma("act", b_v[T // 2:T, :], vtok[T // 2:T, g256])
        P.dma("sp", fl3(b_hq), hgT[g256, :])
        P.dma("act", fl3(b_hk), hgT[ds(i * 256 + 2048, 256), :])
        P.dma("sp", fl3(b_hg), hgT[ds(i * 256 + 4096, 256), :])
        P.dma("act", b_i[0:T // 2, :], itok[0:T // 2, g256])
        P.dma("sp", b_i[T // 2:T, :], itok[T // 2:T, g256])
        P.dma("act", fl3(b_gg), ggT[g256, :])
        P.dma("sp", b_nw[:, :], nwa[i])
        P.barrier()
        emit_B(C, T, b_q, b_k, b_v, b_hq, b_hk, b_hg, b_i, b_gg, b_nw, cst, b_m[0:2], b_m[2:4])
        P.barrier()
        P.dma("sp", mTp[g256, 2:TP], fl3(b_m[0:2]))
        P.dma("act", mTp[ds(i * 256 + SBW, 256), 2:TP], fl3(b_m[2:4]))
        C.end_body()
    with nc.Fori(0, NTILE) as i:
        C.begin()
        P = C.P
        P.dma("sp", c_m[:, :], mTp[:, ds(i * NT, NTH)])
        P.dma("act", c_h[:, :], hpad[:, ds(i * NT, NTH)])
        P.barrier()
        emit_C(C, c_m, c_h, w_out, n2w, w_up, cw, cbv, w_down, c_o, hmid, aT)
        P.barrier()
        P.dma("sp", h2T[:, ds(i * NT, NT)], c_o[:, :])
        C.end_body()
    C.st.close()
    return nc


_PROGS = {}


def _prog(name):
    if name not in _PROGS:
        _PROGS[name] = {"A": build_A, "B": build_B, "C": build_C, "D": build_D, "L": build_L}[name]()
    return _PROGS[name]


def _fm(v):
    return np.ascontiguousarray(np.asarray(v, np.float32).reshape(-1, 128).T)


def layer_inputs(l, hT, norm1_w, w_in, sb_norm_w, hg_lb_param, hg_norm_w, w_out, norm2_w, w_up, conv_w, conv_b, w_down):
    sel = np.zeros(DEPTH, np.float32)
    sel[1:l + 1] = 1.0
    sbw = np.asarray(sb_norm_w[l], np.float32).reshape(16, 128)
    hgw = np.asarray(hg_norm_w[l], np.float32).reshape(16, 128)
    nwa = np.stack([np.stack([sbw[2 * c], sbw[2 * c + 1], hgw[2 * c], hgw[2 * c + 1]], axis=1) for c in range(NCORE)], axis=0)
    d = {"hpad": np.ascontiguousarray(np.concatenate([np.zeros((D, 2), np.float32), hT], axis=1)),
         "w_in": np.ascontiguousarray(np.asarray(w_in[l], np.float32)),
         "n1w": _fm(norm1_w[l]),
         "lbp": np.ascontiguousarray(np.asarray(hg_lb_param, np.float32).reshape(DEPTH, 16, 128).transpose(2, 1, 0)),
         "lsel": np.ascontiguousarray(np.broadcast_to(sel, (128, 16, DEPTH))),
         "nwa": np.ascontiguousarray(nwa.astype(np.float32)),
         "zpad": np.zeros((D, 2), ml_dtypes.bfloat16),
         "w_out": np.ascontiguousarray(np.asarray(w_out[l], np.float32)),
         "n2w": _fm(norm2_w[l]),
         "w_up": np.ascontiguousarray(np.asarray(w_up[l], np.float32)),
         "cw": np.ascontiguousarray(np.asarray(conv_w[l], np.float32).reshape(3, 2 * NJ, 128).transpose(2, 1, 0)),
         "cb": np.ascontiguousarray(np.asarray(conv_b[l], np.float32).reshape(2 * NJ, 128).T),
         "w_down": np.ascontiguousarray(np.asarray(w_down[l], np.float32))}
    d.update(mixer_consts())
    return d


def layer_inputs(l, hT, norm1_w, w_in, sb_norm_w, hg_lb_param, hg_norm_w, w_out, norm2_w, w_up, conv_w, conv_b, w_down):
    sel = np.zeros(DEPTH, np.float32)
    sel[1:l + 1] = 1.0
    sbw = np.asarray(sb_norm_w[l], np.float32).reshape(16, 128)
    hgw = np.asarray(hg_norm_w[l], np.float32).reshape(16, 128)
    nwa = np.stack([np.stack([sbw[2 * c], sbw[2 * c + 1], hgw[2 * c], hgw[2 * c + 1]], axis=1) for c in range(NCORE)], axis=0)
    d = {"hpad": np.ascontiguousarray(np.concatenate([np.zeros((D, 2), np.float32), hT], axis=1)),
         "w_in": np.ascontiguousarray(np.asarray(w_in[l], np.float32)),
         "n1w": _fm(norm1_w[l]),
         "lbp": np.ascontiguousarray(np.asarray(hg_lb_param, np.float32).reshape(DEPTH, 16, 128).transpose(2, 1, 0)),
         "lsel": np.ascontiguousarray(np.broadcast_to(sel, (128, 16, DEPTH))),
         "nwa": np.ascontiguousarray(nwa.astype(np.float32)),
         "zpad": np.zeros((D, 2), ml_dtypes.bfloat16),
         "w_out": np.ascontiguousarray(np.asarray(w_out[l], np.float32)),
         "n2w": _fm(norm2_w[l]),
         "w_up": np.ascontiguousarray(np.asarray(w_up[l], np.float32)),
         "cw": np.ascontiguousarray(np.asarray(conv_w[l], np.float32).reshape(3, 2 * NJ, 128).transpose(2, 1, 0)),
         "cb": np.ascontiguousarray(np.asarray(conv_b[l], np.float32).reshape(2 * NJ, 128).T),
         "w_down": np.ascontiguousarray(np.asarray(w_down[l], np.float32))}
    d.update(mixer_consts())
    return d


def kernel(x, norm1_w, w_in, sb_norm_w, hg_lb_param, hg_norm_w, w_out, norm2_w, w_up, conv_w, conv_b,
           w_down, final_norm_w):
    x = np.asarray(x, np.float32)
    hT = np.ascontiguousarray(x[0].T)
    for l in range(DEPTH):
        ins = layer_inputs(l, hT, norm1_w, w_in, sb_norm_w, hg_lb_param, hg_norm_w, w_out, norm2_w, w_up, conv_w, conv_b, w_down)
        r = run_bass_kernel_spmd(_prog("L"), [ins], core_ids=[0]).results
        hT = r[0]["h2T"]
        del ins, r
    cores = list(range(NCORE))
    fw = _fm(final_norm_w)
    ins = [{"hT": np.ascontiguousarray(hT[:, c * NT:(c + 1) * NT]), "fw": fw} for c in cores]
    rd = run_bass_kernel_spmd(_prog("D"), ins, core_ids=cores).results
    oT = np.concatenate([r["oT"] for r in rd], axis=1)
    return np.ascontiguousarray(oT.T)[None].astype(np.float32)
```
